# Optimizing a Trainium2 kernel written in Bass

```python
import math
import jax, jax.numpy as jnp
from jax import lax
import numpy as np

D_MODEL = 2048
BATCH = 4
SEQ = 8192
DEPTH = 1

D_BRANCH = D_MODEL
CONV_WIDTH = 3
HEAD_SIZE = 64
N_RWKV_HEADS = D_BRANCH // HEAD_SIZE
DECAY_LORA = max(32, int(round(1.8 * math.sqrt(D_MODEL) / 32)) * 32)
ICLR_LORA = max(32, int(round(1.8 * math.sqrt(D_MODEL) / 32)) * 32)
GATE_LORA = max(32, int(round(0.6 * D_MODEL ** 0.8 / 32)) * 32)
D_FF = -(-8 * D_MODEL // (3 * 256)) * 256
ALPHA = (2.0 * DEPTH) ** 0.25
BETA = (8.0 * DEPTH) ** -0.25
LN_EPS = 1e-5
GN_EPS = 64e-5

CONV_LO = 0
RWKV_LO = 3 * D_BRANCH
RWKV_HI = RWKV_LO + 3 * D_BRANCH + DECAY_LORA + ICLR_LORA + GATE_LORA
D_IN_PROJ = RWKV_HI + 2 * D_BRANCH
D_SHIFT = RWKV_HI - RWKV_LO
RWKV_SPLITS = (D_BRANCH, 2 * D_BRANCH, 3 * D_BRANCH,
               3 * D_BRANCH + DECAY_LORA, 3 * D_BRANCH + DECAY_LORA + ICLR_LORA)

kernel_name = "hybrid_shortconv_rwkv7_gated_deepnorm"


def _layer_norm(x, g, b, eps):
    xf = x.astype(jnp.float32)
    mu = jnp.mean(xf, axis=-1, keepdims=True)
    var = jnp.mean(jnp.square(xf - mu), axis=-1, keepdims=True)
    y = (xf - mu) * lax.rsqrt(var + eps) * g.astype(jnp.float32) + b.astype(jnp.float32)
    return y.astype(x.dtype)


def _token_shift(z, mu):
    z_prev = jnp.pad(z, ((0, 0), (1, 0), (0, 0)))[:, :-1]
    return z + mu * (z_prev - z)


def _causal_depthwise_conv(u, w):
    return lax.conv_general_dilated(
        u, w[:, None, :].astype(u.dtype), window_strides=(1,),
        padding=[(CONV_WIDTH - 1, 0)], dimension_numbers=("NWC", "WIO", "NWC"),
        feature_group_count=u.shape[-1])


def _wkv7(r, w, k, v, a, b):
    bsz, _, h, n = r.shape

    def step(S, inp):
        r_t, w_t, k_t, v_t, a_t, b_t = inp
        sa = jnp.einsum("bhvk,bhk->bhv", S, a_t)
        S = (S * w_t[:, :, None, :] + sa[..., None] * b_t[:, :, None, :]
             + v_t[..., None] * k_t[:, :, None, :])
        y_t = jnp.einsum("bhvk,bhk->bhv", S, r_t)
        return S, y_t

    xs = tuple(jnp.moveaxis(t, 1, 0) for t in (r, w, k, v, a, b))
    S0 = jnp.zeros((bsz, h, n, n), jnp.float32)
    _, y = lax.scan(step, S0, xs)
    return jnp.moveaxis(y, 0, 1)


def _rwkv7_branch(z, w0, w_up, a0, a_up, g_up, k_k, k_a, r_k, gn_g, gn_b):
    f32 = jnp.float32
    r, k, v, wd, ad, gd = jnp.split(z, RWKV_SPLITS, axis=-1)
    bsz, t, _ = r.shape

    def heads(u):
        return u.astype(f32).reshape(bsz, t, N_RWKV_HEADS, HEAD_SIZE)

    w_log = -jax.nn.softplus(-(w0 + jnp.tanh(wd) @ w_up).astype(f32)) - 0.5
    decay = jnp.exp(-jnp.exp(w_log))
    a = jax.nn.sigmoid((a0 + ad @ a_up).astype(f32))
    g = jax.nn.sigmoid(gd) @ g_up
    kk = heads(k * k_k)
    kk = kk / jnp.maximum(jnp.sqrt(jnp.sum(kk * kk, axis=-1, keepdims=True)), 1e-12)
    k_mod = heads(k.astype(f32) * (1.0 + (a - 1.0) * k_a.astype(f32)))
    rh, vh = heads(r), heads(v)
    y = _wkv7(rh, heads(decay), k_mod, vh, -kk, kk * heads(a))
    mu = jnp.mean(y, axis=-1, keepdims=True)
    var = jnp.mean(jnp.square(y - mu), axis=-1, keepdims=True)
    yn = ((y - mu) * lax.rsqrt(var + GN_EPS)).reshape(bsz, t, D_BRANCH)
    yn = yn * gn_g.astype(f32) + gn_b.astype(f32)
    bonus = jnp.sum(rh * k_mod * r_k.astype(f32), axis=-1, keepdims=True) * vh
    out = (yn + bonus.reshape(bsz, t, D_BRANCH)) * g.astype(f32)
    return out.astype(z.dtype)


def setup_inputs(seed: int = 0) -> dict:
    key = jax.random.key(seed)
    ks = jax.random.split(key, 22)
    nrm = jax.random.normal
    L, D = DEPTH, D_MODEL
    f = jnp.float32
    return {
        "x": nrm(ks[0], (BATCH, SEQ, D), f),
        "w_in": nrm(ks[1], (L, D, D_IN_PROJ), f) * D ** -0.5,
        "shift_mu": jax.random.uniform(ks[2], (L, D_SHIFT), f),
        "conv_w": nrm(ks[3], (L, CONV_WIDTH, D_BRANCH), f) * CONV_WIDTH ** -0.5,
        "w0": jax.random.uniform(ks[4], (L, D_BRANCH), f, -6.0, 1.0),
        "w_up": nrm(ks[5], (L, DECAY_LORA, D_BRANCH), f) * 0.5 * DECAY_LORA ** -0.5,
        "a0": nrm(ks[6], (L, D_BRANCH), f) * 0.5,
        "a_up": nrm(ks[7], (L, ICLR_LORA, D_BRANCH), f) * 0.5 * ICLR_LORA ** -0.5,
        "g_up": nrm(ks[8], (L, GATE_LORA, D_BRANCH), f) * GATE_LORA ** -0.5,
        "k_k": 0.85 + 0.05 * nrm(ks[9], (L, D_BRANCH), f),
        "k_a": 1.0 + 0.05 * nrm(ks[10], (L, D_BRANCH), f),
        "r_k": 0.1 * nrm(ks[11], (L, N_RWKV_HEADS, HEAD_SIZE), f),
        "gn_g": 1.0 + 0.02 * nrm(ks[12], (L, D_BRANCH), f),
        "gn_b": 0.02 * nrm(ks[13], (L, D_BRANCH), f),
        "w_o": nrm(ks[14], (L, D_BRANCH, D), f) * D_BRANCH ** -0.5 * BETA,
        "ln1_g": 1.0 + 0.02 * nrm(ks[15], (L, D), f),
        "ln1_b": 0.02 * nrm(ks[16], (L, D), f),
        "w_gu": nrm(ks[17], (L, D, 2 * D_FF), f) * D ** -0.5,
        "w_down": nrm(ks[18], (L, D_FF, D), f) * D_FF ** -0.5 * BETA,
        "ln2_g": 1.0 + 0.02 * nrm(ks[19], (L, D), f),
        "ln2_b": 0.02 * nrm(ks[20], (L, D), f),
    }


def reference(x, w_in, shift_mu, conv_w, w0, w_up, a0, a_up, g_up, k_k, k_a, r_k,
              gn_g, gn_b, w_o, ln1_g, ln1_b, w_gu, w_down, ln2_g, ln2_b):
    for i in range(DEPTH):
        proj = x @ w_in[i]
        c_b = proj[..., CONV_LO:CONV_LO + D_BRANCH]
        c_c = proj[..., CONV_LO + D_BRANCH:CONV_LO + 2 * D_BRANCH]
        c_h = proj[..., CONV_LO + 2 * D_BRANCH:RWKV_LO]
        y_conv = c_b * _causal_depthwise_conv(c_c * c_h, conv_w[i])

        z = _token_shift(proj[..., RWKV_LO:RWKV_HI], shift_mu[i])
        y_rwkv = _rwkv7_branch(z, w0[i], w_up[i], a0[i], a_up[i], g_up[i], k_k[i], k_a[i],
                               r_k[i], gn_g[i], gn_b[i])

        gate_conv = proj[..., RWKV_HI:RWKV_HI + D_BRANCH]
        gate_rwkv = proj[..., RWKV_HI + D_BRANCH:]
        merged = jax.nn.sigmoid(gate_conv) * y_conv + jax.nn.sigmoid(gate_rwkv) * y_rwkv
        x = _layer_norm(ALPHA * x + merged @ w_o[i], ln1_g[i], ln1_b[i], LN_EPS)

        gate, up = jnp.split(x @ w_gu[i], 2, axis=-1)
        x = _layer_norm(ALPHA * x + (jax.nn.silu(gate) * up) @ w_down[i],
                        ln2_g[i], ln2_b[i], LN_EPS)
    return x
```

```python
import math
from contextlib import ExitStack
import numpy as np
import ml_dtypes
import concourse.bass as bass
import concourse.mybir as mybir
from concourse.bass_utils import run_bass_kernel_spmd

F32, BF16 = mybir.dt.float32, mybir.dt.bfloat16
AF, ALU = mybir.ActivationFunctionType, mybir.AluOpType

D = 2048
TT = 256
NCH = TT // 64
NKC = 16
NPAIR = 16
GP = 8
DFF = 5632
NFC = 44
SEQ = 8192
ALPHA = 2.0 ** 0.25
LN_EPS = 1e-5
GN_EPS = 64e-5
CNEG = -math.exp(-0.5)
RWKV_LO = 3 * D
RWKV_HI = RWKV_LO + 3 * D + 96 + 96 + 256
T_WD, T_AD, T_GD0, T_GD1 = 0, 1, 2, 3
def T_RKV(q, j): return 4 + 3 * q + j
def T_CONV(q, j): return 52 + 5 * q + j
def T_WO(m): return 132 + m
def T_GU(f, j): return 148 + 2 * f + j
N_WA = 236
NPAR = 16


class Sched:
    def __init__(self):
        self.ops = []

    KEYMAP = {"M2": "E1", "VAR": "E2", "SD": "E3", "RS": "E4", "TQ": "GX", "CCS": "ZV", "ACC": "LW",
              "YC": "AL", "SGC": "SN", "SGR": "RN", "MEAN": "KKN", "SG0": "T1", "SG1": "KM", "X1B": "XB"}
    LANEKEYS = {"RW0", "RW1", "RW2", "DT", "ZR", "ZK", "ZV", "LW", "GS", "GX", "E1", "E2", "E3", "E4", "AL", "SN",
                "RN", "KKN", "T1", "KM", "KA", "SQB", "EB", "YB", "Y2B", "UC"}
    lane = 0

    def _key(self, k):
        k = self.KEYMAP.get(k, k)
        return k + f"@{self.lane}" if k in self.LANEKEYS else k

    def add(self, eng, fn, reads, writes, dma=False):
        self.ops.append(dict(eng=eng, fn=fn, reads=tuple(self._key(k) for k in reads),
                             writes=tuple(self._key(k) for k in writes), dma=dma))

    def emit(self, nc, es):
        ops = self.ops
        engs = ['pe', 'act', 'dve', 'pool', 'sp']
        last_w, readers = {}, {}
        waited = {e: {f: -1 for f in engs} for e in engs}
        waited_dma = {e: set() for e in engs}
        for i, op in enumerate(ops):
            deps = set()
            for k in op['reads']:
                if k in last_w:
                    deps.add(last_w[k])
            for k in op['writes']:
                if k in last_w:
                    deps.add(last_w[k])
                deps.update(readers.get(k, ()))
            for k in op['reads']:
                readers.setdefault(k, []).append(i)
            for k in op['writes']:
                last_w[k] = i
                readers[k] = []
            deps.discard(i)
            e = op['eng']
            w_eng, w_dma = {}, []
            for j in deps:
                oj = ops[j]
                if oj['dma']:
                    if j not in waited_dma[e]:
                        waited_dma[e].add(j)
                        w_dma.append(j)
                else:
                    f = oj['eng']
                    if f == e and e == 'pe':
                        continue
                    if j > waited[e][f]:
                        w_eng[f] = max(w_eng.get(f, -1), j)
            for f, j in w_eng.items():
                waited[e][f] = j
                ops[j]['sig'] = True
            op['w_eng'] = w_eng
            op['w_dma'] = w_dma
        cnt = {e: 0 for e in engs}
        for op in ops:
            if op.get('sig') and not op['dma']:
                cnt[op['eng']] += 1
                op['signum'] = cnt[op['eng']]
        sig_sem = {e: es.enter_context(nc.semaphore("sg_" + e)) for e in ['pe', 'act', 'dve', 'pool']}
        NDS = 24
        dsem = {q: [es.enter_context(nc.semaphore(f"d_{q}{i}")) for i in range(NDS)] for q in ['sp', 'pool']}
        dcount = {'sp': 0, 'pool': 0}
        for op in ops:
            if op['dma']:
                q = op['eng']
                k = dcount[q]
                dcount[q] += 1
                op['dsem'] = dsem[q][k % NDS]
                op['dval'] = 16 * (k // NDS + 1)
        per_eng = {e: [op for op in ops if op['eng'] == e] for e in engs}
        all_sems = list(sig_sem.values()) + dsem['sp'] + dsem['pool']
        nc.all_engine_barrier()
        for sm in all_sems:
            nc.gpsimd.sem_clear(sm)
        nc.all_engine_barrier()
        self.all_sems = all_sems
        block = es.enter_context(nc.Block())

        def run(eng_handle, ename):
            for op in per_eng[ename]:
                for f, j in op['w_eng'].items():
                    eng_handle.wait_ge(sig_sem[f], ops[j]['signum'])
                for j in op['w_dma']:
                    eng_handle.wait_ge(ops[j]['dsem'], ops[j]['dval'])
                if op['dma'] and op['dval'] > 16:
                    eng_handle.wait_ge(op['dsem'], op['dval'] - 16)
                inst = op['fn'](eng_handle)
                if op['dma']:
                    inst.then_inc(op['dsem'], 16)
                elif op.get('sig'):
                    inst.then_inc(sig_sem[ename], 1)
            if ename in ('sp', 'pool'):
                k = dcount[ename]
                for i in range(min(NDS, k)):
                    n_i = (k - 1 - i) // NDS + 1
                    eng_handle.wait_ge(dsem[ename][i], 16 * n_i)

        @block.tensor
        def _(e):
            run(e, 'pe')

        @block.scalar
        def _(e):
            run(e, 'act')

        @block.vector
        def _(e):
            run(e, 'dve')

        @block.gpsimd
        def _(e):
            run(e, 'pool')

        @block.sync
        def _(e):
            run(e, 'sp')


def build_program(NT):
    nc = bass.Bass("TRN2", target_bir_lowering=False)
    es = ExitStack()
    P = Sched()

    def dram(name, shape, dt, kind):
        return nc.dram_tensor(name, shape, dt, kind=kind).ap()

    xo = dram("xo", [NT, 128, NKC * TT], F32, "ExternalInput")
    xp = dram("xp", [NT, 128, NKC * TT], F32, "ExternalInput")
    wa = dram("wa", [N_WA, 128, 2048], F32, "ExternalInput")
    wdn = dram("wdn", [16, 128, DFF], F32, "ExternalInput")
    pa_d = dram("pa", [128, NPAIR * NPAR], F32, "ExternalInput")
    pl_d = dram("pl", [128, 4], F32, "ExternalInput")
    pln_d = dram("pln", [128, 64], F32, "ExternalInput")
    lup_d = dram("lup", [128, 4 * NPAIR * 128], F32, "ExternalInput")
    cst_d = dram("cst", [128, 128 * 3 + 512 + 512 + TT + 256], BF16, "ExternalInput")
    outT = dram("outT", [NT, 128, NKC * TT], F32, "ExternalOutput")
    wab = dram("wab", [N_WA, 128, 2048], BF16, "Internal")
    wdb = dram("wdb", [16, 128, DFF], BF16, "Internal")

    def sb(name, shape, dt):
        return es.enter_context(nc.sbuf_tensor(name, shape, dt))

    def ps(name, shape, dt):
        return es.enter_context(nc.psum_tensor(name, shape, dt))

    PA = sb("PA", [128, NPAIR, NPAR], F32)
    PL = sb("PL", [128, 4], F32)
    PLN = sb("PLN", [128, 4, 16], F32)
    LUP = sb("LUP", [128, 4, NPAIR, 128], BF16)
    CST = sb("CST", [128, 128 * 3 + 512 + 512 + TT + 256], BF16)
    IDENT = CST[:, 0:128]
    BONES = CST[:, 128:256]
    LONES = CST[:, 256:384]
    MASKP = CST[:, 384:896]
    MASKT = CST[:, 896:1408]
    RESETb = CST[:, 1408:1408 + TT]
    IDENT4 = CST[:, 1408 + TT:1408 + TT + 256]

    RESET = sb("RESET", [128, TT], F32)
    R = sb("R", [128, NKC, TT], F32)
    RK = tuple(f"R{m}" for m in range(NKC))
    XB = sb("XB", [128, NKC, TT], BF16)
    NWB = 3
    WB = [sb(f"WB{i}", [128, NKC, 128], BF16) for i in range(NWB)]
    WDB = [sb(f"WDB{i}", [128, 11, 128], BF16) for i in range(3)]
    CR = sb("CR", [128, NPAIR * 3 + 4], F32)
    UCR = sb("UCR", [128, NPAIR, 2], F32)
    S32 = sb("S32", [128, NPAIR, 64], F32)
    SBF = sb("SBF", [128, NPAIR, 64], BF16)
    GC = sb("GC", [128, GP, NCH], F32)
    RWL = sb("RWL", [128, TT + 1], F32)
    TWD = sb("TWD", [128, TT], BF16)
    ZAD = sb("ZAD", [128, TT], BF16)
    SGD = sb("SGD", [128, 2, TT], BF16)
    class LT:
        def __init__(self, ts_):
            self.ts_ = ts_
        def __getitem__(self, idx):
            return self.ts_[P.lane][idx]

    def t32(name, n=TT):
        return LT([sb(name, [128, n], F32), sb(name + "_b", [128, n], F32)])
    RW = [t32(f"RW{j}", TT + 1) for j in range(3)]
    DT_ = t32("DT")
    ZR, ZK, ZV = t32("ZR"), t32("ZK"), t32("ZV")
    LW, GS, GX = t32("LW"), t32("GS"), t32("GX")
    E1, E2, E3, E4 = t32("E1"), t32("E2"), t32("E3"), t32("E4")
    AL, SN, RN, KKN, T1, KM, KA = t32("AL"), t32("SN"), t32("RN"), t32("KKN"), t32("T1"), t32("KM"), t32("KA")
    SQB = LT([sb("SQB", [128, TT], BF16), sb("SQB_b", [128, TT], BF16)])
    EB = LT([sb("EB", [128, TT], BF16), sb("EB_b", [128, TT], BF16)])
    ARENA = sb("ARENA", [128, 7 * GP * TT], BF16)
    AR = ARENA[:, 0:2 * GP * TT].rearrange("p (a b c) -> p a b c", a=GP, b=NCH)
    def _ar(j):
        return ARENA[:, (2 + j) * GP * TT:(3 + j) * GP * TT].rearrange("p (a b) -> p a b", a=GP)
    BT, KT, BH, KH, VB = (_ar(j) for j in range(5))
    BON = sb("BON", [128, GP, TT], F32)
    GG = sb("GG", [128, GP, TT], F32)
    YG = sb("YG", [128, GP, TT], F32)
    TPS = sb("TPS", [128, GP, 3, 64], BF16)
    NB = sb("NB", [128, GP, 128], BF16)
    NK = sb("NK", [128, GP, 128], BF16)
    NT0 = sb("NT0", [128, GP, 64], BF16)
    NN = [sb(f"NN{j}", [128, GP, 128], BF16) for j in range(5)]
    XA = [sb(f"XA{j}", [128, GP, 64], BF16) for j in range(2)]
    YB = LT([sb("YB", [128, TT], BF16), sb("YB_b", [128, TT], BF16)])
    Y2B = LT([sb("Y2B", [128, TT], BF16), sb("Y2B_b", [128, TT], BF16)])
    M2, VAR, SD, RS, TQ = E1, E2, E3, E4, GX
    CCS = ZV
    UC = t32("UC", TT + 2)
    ACC, YC, SGC, SGR = LW, AL, SN, RN
    MT = sb("MT", [128, NPAIR, TT], BF16)
    HB = sb("HB", [128, 2, TT], BF16)
    HS = sb("HS", [128, 2, TT], BF16)
    MEAN = KKN
    X1B = XB
    SG = [T1, KM]
    ACTB = ARENA[:, 0:NFC * TT].rearrange("p (a b) -> p a b", a=NFC)
    OAK = tuple(f"OA{ql}" for ql in range(GP))
    NBK = 8
    BK = [ps(f"BK{i}", [128, 512], F32) for i in range(NBK)]
    bk_i = [0]

    def nb():
        i = bk_i[0] % NBK
        bk_i[0] += 1
        return BK[i], f"BK{i}"

    def dma(q, out, in_, reads, writes, **kw):
        P.add(q, lambda e: e.dma_start(out=out, in_=in_, **kw), reads, writes, dma=True)

    def mm(out, lhsT, rhs, start, stop, reads, writes):
        P.add('pe', lambda e: e.matmul(out, lhsT=lhsT, rhs=rhs, start=start, stop=stop), reads, writes)

    def tr(out, in_, ident, reads, writes):
        P.add('pe', lambda e: e.transpose(out, in_, ident), reads, writes)

    def act(out, in_, func, reads, writes, bias=None, scale=None):
        kw = {}
        if bias is not None:
            kw['bias'] = bias
        if scale is not None:
            kw['scale'] = scale
        P.add('act', lambda e: e.activation(out=out, in_=in_, func=func, **kw), reads, writes)

    def acopy(out, in_, reads, writes):
        P.add('act', lambda e: e.copy(out=out, in_=in_), reads, writes)

    def vcopy(out, in_, reads, writes):
        P.add('dve', lambda e: e.tensor_copy(out=out, in_=in_), reads, writes)

    def tt(out, in0, in1, op, reads, writes):
        P.add('dve', lambda e: e.tensor_tensor(out=out, in0=in0, in1=in1, op=op), reads, writes)

    def ts(out, in0, s1, s2, op0, op1, reads, writes):
        if op1 is None:
            P.add('dve', lambda e: e.tensor_scalar(out=out, in0=in0, scalar1=s1, scalar2=None, op0=op0), reads, writes)
        else:
            P.add('dve', lambda e: e.tensor_scalar(out=out, in0=in0, scalar1=s1, scalar2=s2, op0=op0, op1=op1), reads, writes)

    def stt(out, in0, scalar, in1, op0, op1, reads, writes):
        P.add('dve', lambda e: e.scalar_tensor_tensor(out=out, in0=in0, scalar=scalar, in1=in1, op0=op0, op1=op1), reads, writes)

    def ptt(out, in0, in1, op, reads, writes):
        P.add('pool', lambda e: e.tensor_tensor(out=out, in0=in0, in1=in1, op=op), reads, writes)

    def pts(out, in0, s1, s2, op0, op1, reads, writes):
        P.add('pool', lambda e: e.tensor_scalar(out=out, in0=in0, scalar1=s1, scalar2=s2, op0=op0, op1=op1), reads, writes)

    def pcopy(out, in_, reads, writes):
        P.add('pool', lambda e: e.tensor_copy(out=out, in_=in_), reads, writes)

    def recip(out, in_, reads, writes):
        P.add('dve', lambda e: e.reciprocal(out=out, in_=in_), reads, writes)

    def memset(ap, val, writes):
        P.add('dve', lambda e: e.memset(ap, val), (), writes)

    def conv_w(t):
        dma('pool', wab[t], wa[t], (), (f"wab{t}",), max_dma_last_dim=8192)

    def conv_wd(m):
        dma('pool', wdb[m], wdn[m], (), (f"wdb{m}",), max_dma_last_dim=8192)

    order = [T_WD, T_AD]
    for q in range(NPAIR):
        order += [T_RKV(q, 1), T_RKV(q, 2)]
    order += [T_GD0, T_GD1]
    for g in range(NPAIR // GP):
        for q in range(g * GP, (g + 1) * GP):
            order.append(T_RKV(q, 0))
        for q in range(g * GP, (g + 1) * GP):
            order += [T_CONV(q, j) for j in (1, 2, 0, 3, 4)]
    order += [T_WO(m) for m in range(16)]
    for f in range(NFC):
        order += [T_GU(f, 0), T_GU(f, 1)]
    assert sorted(order) == list(range(N_WA))
    for t in order:
        conv_w(t)
    for m in range(16):
        conv_wd(m)

    dma('sp', PA[:].rearrange("p a b -> p (a b)"), pa_d, (), ("PA",))
    dma('sp', PL[:], pl_d, (), ("PL",))
    dma('sp', PLN[:].rearrange("p a b -> p (a b)"), pln_d, (), ("PLN",))
    dma('sp', CST[:], cst_d, (), ("CST",))
    for j in range(4):
        Rf = R[:].rearrange("p a b -> p (a b)")[:, 0:NPAIR * 128]
        dma('sp', Rf, lup_d[:, j * NPAIR * 128:(j + 1) * NPAIR * 128], (), RK)
        acopy(LUP[:, j].rearrange("p a b -> p (a b)"), Rf, RK, ("LUP",))
    vcopy(RESET[:], RESETb, ("CST",), ("RESET",))
    ts(PA[:, :, 13], PA[:, :, 6], -1.0, 1.0, ALU.mult, ALU.add, ("PA",), ("PA",))
    memset(CR[:], 0.0, tuple(f"CR{j}" for j in range(NPAIR * 3 + 4)))
    memset(UCR[:].rearrange("p a b -> p (a b)"), 0.0, tuple(f"UCR{q}" for q in range(NPAIR)))
    memset(S32[:].rearrange("p a b -> p (a b)"), 0.0, ("S3200", "S3201", "S3210", "S3211"))
    memset(SBF[:].rearrange("p a b -> p (a b)"), 0.0, ("SBF00", "SBF01", "SBF10", "SBF11"))

    wb_i = [0]

    def load_w(t):
        i = wb_i[0] % NWB
        wb_i[0] += 1
        dma('sp', WB[i][:].rearrange("p a b -> p (a b)"), wab[t], (f"wab{t}",), (f"WB{i}",))
        return WB[i], f"WB{i}"

    def gemm16(t, rhs_fn, rhs_keys, M=128):
        w, wk = load_w(t)
        bk, bkk = nb()
        for kc in range(NKC):
            mm(bk[:M, 0:TT], w[:, kc, 0:M], rhs_fn(kc), kc == 0, kc == NKC - 1, (wk,) + tuple(rhs_keys), (bkk,))
        return bk, bkk

    xb_rhs = lambda kc: XB[:, kc, :]

    def shift(bk, bkk, rw, rwk, slot, mu_ap, z, zk_, M=128):
        acopy(rw[:M, 1:TT + 1], bk[:M, 0:TT], (bkk,), (rwk,))
        pcopy(rw[:M, 0:1], CR[:M, slot:slot + 1], (f"CR{slot}",), (rwk,))
        pcopy(CR[:M, slot:slot + 1], rw[:M, TT:TT + 1], (rwk,), (f"CR{slot}",))
        ptt(DT_[:M, :], rw[:M, 0:TT], rw[:M, 1:TT + 1], ALU.subtract, (rwk,), ("DT",))
        stt(z, DT_[:M, :], mu_ap, rw[:M, 1:TT + 1], ALU.mult, ALU.add, ("DT", rwk, "PA", "PL"), (zk_,))

    c4 = lambda ap: ap.rearrange("p (c t) -> p c t", c=NCH)

    def carries_only():
        for j in range(2):
            bk, bkk = gemm16(T_GD0 + j, xb_rhs, ("XB",))
            vcopy(CR[:, NPAIR * 3 + 2 + j:NPAIR * 3 + 3 + j], bk[:, TT - 1:TT], (bkk,), (f"CR{NPAIR * 3 + 2 + j}",))
        for q in range(NPAIR):
            bk, bkk = gemm16(T_RKV(q, 0), xb_rhs, ("XB",))
            vcopy(CR[:, q * 3:q * 3 + 1], bk[:, TT - 1:TT], (bkk,), (f"CR{q * 3}",))
            bk, bkk = gemm16(T_CONV(q, 1), xb_rhs, ("XB",))
            acopy(CCS[:, 0:2], bk[:, TT - 2:TT], (bkk,), ("CCS",))
            bk, bkk = gemm16(T_CONV(q, 2), xb_rhs, ("XB",))
            tt(UCR[:, q, :], CCS[:, 0:2], bk[:, TT - 2:TT], ALU.mult, ("CCS", bkk), (f"UCR{q}",))

    def do_tile(xsrc, i, own, last_pre=False):
        dma('sp', R[:].rearrange("p a b -> p (a b)"), xsrc[i], (), RK)
        for kc in range(NKC):
            if kc % 2 == 0:
                acopy(XB[:, kc, :], R[:, kc, :], (f"R{kc}",), ("XB",))
            else:
                vcopy(XB[:, kc, :], R[:, kc, :], (f"R{kc}",), ("XB",))
        bk, bkk = gemm16(T_WD, xb_rhs, ("XB",))
        shift(bk, bkk, RWL, "RWL", NPAIR * 3 + 0, PL[:, 0:1], DT_[:, :], "DT")
        act(TWD[:], DT_[:, :], AF.Tanh, ("DT",), ("TWD",))
        bk, bkk = gemm16(T_AD, xb_rhs, ("XB",))
        shift(bk, bkk, RWL, "RWL", NPAIR * 3 + 1, PL[:, 1:2], DT_[:, :], "DT")
        acopy(ZAD[:], DT_[:, :], ("DT",), ("ZAD",))
        if own:
            for j in range(2):
                bk, bkk = gemm16(T_GD0 + j, xb_rhs, ("XB",))
                shift(bk, bkk, RWL, "RWL", NPAIR * 3 + 2 + j, PL[:, 2 + j:3 + j], DT_[:, :], "DT")
                act(SGD[:, j, :], DT_[:, :], AF.Sigmoid, ("DT",), ("SGD",))
        def lockstep(items):
            items = list(items)
            while items:
                for it in list(items):
                    P.lane = it[0]
                    try:
                        next(it[1])
                    except StopIteration:
                        items.remove(it)
            P.lane = 0

        for g in range(NPAIR // GP):
            for ql in range(0, GP, 2):
                lockstep([(ln, prep_pair(g * GP + ql + ln, ql + ln, own)) for ln in range(2)])
            for c in range(NCH):
                scan_chunk(g, c, own)
            if own:
                for ql in range(0, GP, 2):
                    lockstep([(ln, post_pair(g * GP + ql + ln, ql + ln)) for ln in range(2)])
        if own:
            ffn_tile(i)
        if last_pre:
            carries_only()

    def prep_pair(q, ql, own):
        par = lambda j: PA[:, q, j:j + 1]
        oak = f"OA{ql}"
        if own:
            bk, bkk = gemm16(T_RKV(q, 0), xb_rhs, ("XB",))
            shift(bk, bkk, RW[0], "RW0", q * 3 + 0, par(0), ZR[:], "ZR")
            yield
        bk, bkk = gemm16(T_RKV(q, 1), xb_rhs, ("XB",))
        shift(bk, bkk, RW[1], "RW1", q * 3 + 1, par(1), ZK[:], "ZK")
        yield
        bk, bkk = gemm16(T_RKV(q, 2), xb_rhs, ("XB",))
        shift(bk, bkk, RW[2], "RW2", q * 3 + 2, par(2), ZV[:], "ZV")
        yield
        bk, bkk = nb()
        mm(bk[:, 0:TT], LUP[:, 0, q, :], TWD[:], True, True, ("LUP", "TWD"), (bkk,))
        act(LW[:], bk[:, 0:TT], AF.Sigmoid, (bkk, "PA"), ("LW",), bias=par(3))
        yield
        gs_, lw_ = GS[:], LW[:]
        P.add('dve', lambda e: e.tensor_tensor_scan(out=gs_, data0=RESET[:], data1=lw_, initial=0.0,
                                                     op0=ALU.mult, op1=ALU.add), ("RESET", "LW"), ("GS",))
        yield
        act(E1[:], GS[:], AF.Exp, ("GS",), ("E1",), scale=CNEG)
        act(E2[:], GS[:], AF.Exp, ("GS",), ("E2",), scale=-CNEG)
        yield
        ptt(GX[:], GS[:], LW[:], ALU.subtract, ("GS", "LW"), ("GX",))
        act(E3[:], GX[:], AF.Exp, ("GX",), ("E3",), scale=CNEG)
        yield
        for c in range(NCH):
            pts(E4[:, c * 64:(c + 1) * 64], E2[:, c * 64:(c + 1) * 64], E1[:, c * 64 + 63:c * 64 + 64], 1.0,
                ALU.mult, ALU.mult, ("E1", "E2"), ("E4",))
        pcopy(GC[:, ql, :], c4(E1[:])[:, :, 63], ("E1",), (f"GC{ql}",))
        yield
        bk, bkk = nb()
        mm(bk[:, 0:TT], LUP[:, 1, q, :], ZAD[:], True, True, ("LUP", "ZAD"), (bkk,))
        act(AL[:], bk[:, 0:TT], AF.Sigmoid, (bkk, "PA"), ("AL",), bias=par(4))
        yield
        act(SQB[:], ZK[:], AF.Square, ("ZK", "PA"), ("SQB",), scale=par(5))
        bk, bkk = nb()
        mm(bk[:, 0:TT], BONES, SQB[:], True, True, ("CST", "SQB"), (bkk,))
        act(SN[:], bk[:, 0:TT], AF.Sqrt, (bkk,), ("SN",), scale=64.0, bias=1e-30)
        yield
        ts(SN[:], SN[:], 1e-12, None, ALU.max, None, ("SN",), ("SN",))
        recip(RN[:], SN[:], ("SN",), ("RN",))
        yield
        stt(KKN[:], ZK[:], par(5), RN[:], ALU.mult, ALU.mult, ("ZK", "PA", "RN"), ("KKN",))
        yield
        pts(T1[:], AL[:], par(6), par(13), ALU.mult, ALU.add, ("AL", "PA"), ("T1",))
        ptt(KM[:], ZK[:], T1[:], ALU.mult, ("ZK", "T1"), ("KM",))
        ptt(KA[:], KKN[:], AL[:], ALU.mult, ("KKN", "AL"), ("KA",))
        yield
        stt(AR[:, ql, :, 0:64], c4(KKN[:]), -1.0, c4(E3[:]), ALU.mult, ALU.mult, ("KKN", "E3"), (oak,))
        if own:
            ptt(AR[:, ql, :, 64:128], c4(ZR[:]), c4(E1[:]), ALU.mult, ("ZR", "E1"), (oak,))
        ptt(BT[:, ql, :], KA[:], E2[:], ALU.mult, ("KA", "E2"), (oak,))
        yield
        tt(KT[:, ql, :], KM[:], E2[:], ALU.mult, ("KM", "E2"), (oak,))
        ptt(BH[:, ql, :], KA[:], E4[:], ALU.mult, ("KA", "E4"), (oak,))
        yield
        tt(KH[:, ql, :], KM[:], E4[:], ALU.mult, ("KM", "E4"), (oak,))
        acopy(VB[:, ql, :], ZV[:], ("ZV",), (oak,))
        yield
        if own:
            stt(EB[:], ZR[:], par(7), KM[:], ALU.mult, ALU.mult, ("ZR", "PA", "KM"), ("EB",))
            bk, bkk = nb()
            mm(bk[:, 0:TT], BONES, EB[:], True, True, ("CST", "EB"), (bkk,))
            stt(BON[:, ql, :], bk[:, 0:TT], 64.0, ZV[:], ALU.mult, ALU.mult, (bkk, "ZV"), (f"BON{ql}",))
            yield
            bk, bkk = nb()
            for j in range(2):
                mm(bk[:, 0:TT], LUP[:, 2 + j, q, :], SGD[:, j, :], j == 0, j == 1, ("LUP", "SGD"), (bkk,))
            acopy(GG[:, ql, :], bk[:, 0:TT], (bkk,), (f"GG{ql}",))
            yield
        yield

    def scan_chunk(g, c, own):
        gens = [scan_gen(g, c, own, 0), scan_gen(g, c, own, 1)]
        while gens:
            for gn in list(gens):
                try:
                    next(gn)
                except StopIteration:
                    gens.remove(gn)

    def scan_gen(g, c, own, sub):
        cs = slice(c * 64, (c + 1) * 64)
        hs = [slice(0, 64), slice(64, 128)]
        qls = list(range(4 * sub, 4 * sub + 4))
        qsl = slice(4 * sub, 4 * sub + 4)
        cpn = [0]

        class _CP:
            def __getitem__(self, _):
                cpn[0] += 1
                return vcopy if cpn[0] % 4 == 0 else acopy
        cp = _CP()
        ncol = 128 if own else 64
        K = lambda base, hi: f"{base}{hi}{sub}"
        TPSf = TPS[:].rearrange("p a b c -> p (a b c)")
        for hi, h in enumerate(hs):
            b0, b0n = nb()
            v0 = b0[:, :].bitcast(BF16)
            for i4, ql in enumerate(qls):
                for j, src in enumerate((BH, KH, VB)):
                    o = (i4 * 3 + j) * 64
                    tr(v0[h, o:o + 64], src[h, ql, cs], IDENT[h, h], (f"OA{ql}", "CST"), (b0n,))
            cp[(hi + sub) % 2](TPSf[h, sub * 768:(sub + 1) * 768], v0[h, 0:768], (b0n,), (K("TPS", hi),))
        yield
        for src, dstN, dn in ((BT, NB, "NB"), (KT, NK, "NK")):
            banks = [nb() for _ in hs]
            for i4, ql in enumerate(qls):
                for hi, h in enumerate(hs):
                    b_, bn = banks[hi]
                    o = i4 * 128
                    mm(b_[h, o:o + ncol], src[h, ql, cs], AR[h, ql, c, 0:ncol], True, True, (f"OA{ql}",), (bn,))
            for hi, h in enumerate(hs):
                b_, bn = banks[hi]
                if own:
                    tt(dstN[h, qsl, :].rearrange("p a b -> p (a b)"), b_[h, :], MASKP[h, :], ALU.mult,
                       (bn, "CST"), (K(dn, hi),))
                else:
                    m3 = MASKP[h, :].rearrange("p (a b) -> p a b", a=4)[:, :, 0:64]
                    tt(dstN[h, qsl, 0:64], b_[h, :].rearrange("p (a b) -> p a b", a=4)[:, :, 0:64], m3,
                       ALU.mult, (bn, "CST"), (K(dn, hi),))
            yield
        pts = [nb() for _ in hs]
        for i4, ql in enumerate(qls):
            for hi, h in enumerate(hs):
                mm(pts[hi][0][h, i4 * 64:(i4 + 1) * 64], AR[h, ql, c, 0:64], BT[h, ql, cs], True, True, (f"OA{ql}",), (pts[hi][1],))
        for hi, h in enumerate(hs):
            tt(NT0[h, qsl, :].rearrange("p a b -> p (a b)"), pts[hi][0][h, 0:256], MASKT[h, 0:256], ALU.mult,
               (pts[hi][1], "CST"), (K("NT0", hi),))
        yield
        A_prev = lambda ql, h: NB[h, ql, 0:64]
        B_prev = lambda ql, h: NT0[h, ql, :]
        prev_keys = lambda hi: (K("NB", hi), K("NT0", hi))
        for lv in range(5):
            banks = [nb() for _ in hs]
            for i4, ql in enumerate(qls):
                for hi, h in enumerate(hs):
                    s_, sn_ = banks[hi]
                    o = i4 * 128
                    mm(s_[h, o:o + 64], B_prev(ql, h), A_prev(ql, h), True, True, prev_keys(hi), (sn_,))
                    mm(s_[h, o + 64:o + 128], A_prev(ql, h), B_prev(ql, h), True, True, prev_keys(hi), (sn_,))
            for hi, h in enumerate(hs):
                s_, sn_ = banks[hi]
                cp[(hi + sub) % 2](NN[lv][h, qsl, :].rearrange("p a b -> p (a b)"), s_[h, :], (sn_,), (K(f"NN{lv}", hi),))
                pA = NB[h, qsl, 0:64] if lv == 0 else NN[lv - 1][h, qsl, 0:64]
                pk = K("NB", hi) if lv == 0 else K(f"NN{lv - 1}", hi)
                ptt(pA, pA, IDENT4[h, :].rearrange("p (a b) -> p a b", a=4), ALU.add, (pk, "CST"), (pk,))
                if lv == 4:
                    pA = NN[4][h, qsl, 0:64]
                    ptt(pA, pA, IDENT4[h, :].rearrange("p (a b) -> p a b", a=4), ALU.add, (K("NN4", hi), "CST"), (K("NN4", hi),))
            A_prev = (lambda lv_: (lambda ql, h: NN[lv_][h, ql, 0:64]))(lv)
            B_prev = (lambda lv_: (lambda ql, h: NN[lv_][h, ql, 64:128]))(lv)
            prev_keys = (lambda lv_: (lambda hi: (K(f"NN{lv_}", hi),)))(lv)
            yield
        zps = [nb() for _ in hs]
        for i4, ql in enumerate(qls):
            q = g * GP + ql
            for hi, h in enumerate(hs):
                zp, zpn = zps[hi]
                mm(zp[h, i4 * 64:(i4 + 1) * 64], AR[h, ql, c, 0:64], SBF[h, q, :], True, False, (f"OA{ql}", K("SBF", hi)), (zpn,))
                mm(zp[h, i4 * 64:(i4 + 1) * 64], NK[h, ql, 0:64], TPS[h, ql, 2, :], False, True, (K("NK", hi), K("TPS", hi)), (zpn,))
        for hi, h in enumerate(hs):
            cp[(hi + sub) % 2](XA[0][h, qsl, :].rearrange("p a b -> p (a b)"), zps[hi][0][h, 0:256], (zps[hi][1],), (K("XA0", hi),))
        yield
        cur = 0
        for lv in range(6):
            aps = [nb() for _ in hs]
            for i4, ql in enumerate(qls):
                for hi, h in enumerate(hs):
                    A_l = NB[h, ql, 0:64] if lv == 0 else NN[lv - 1][h, ql, 0:64]
                    kk_ = (K("NB", hi),) if lv == 0 else (K(f"NN{lv - 1}", hi),)
                    mm(aps[hi][0][h, i4 * 64:(i4 + 1) * 64], A_l, XA[cur][h, ql, :], True, True, kk_ + (K(f"XA{cur}", hi),), (aps[hi][1],))
            for hi, h in enumerate(hs):
                cp[(hi + sub + lv) % 2](XA[1 - cur][h, qsl, :].rearrange("p a b -> p (a b)"), aps[hi][0][h, 0:256],
                                        (aps[hi][1],), (K(f"XA{1 - cur}", hi),))
            cur = 1 - cur
            yield
        UT = XA[cur]
        utk = lambda hi: K(f"XA{cur}", hi)
        if own:
            yps = [nb() for _ in hs]
            for i4, ql in enumerate(qls):
                q = g * GP + ql
                for hi, h in enumerate(hs):
                    yp, ypn = yps[hi]
                    o = slice(i4 * 64, (i4 + 1) * 64)
                    mm(yp[h, o], SBF[h, q, :], AR[h, ql, c, 64:128], True, False, (K("SBF", hi), f"OA{ql}"), (ypn,))
                    mm(yp[h, o], UT[h, ql, :], NB[h, ql, 64:128], False, False, (utk(hi), K("NB", hi)), (ypn,))
                    mm(yp[h, o], TPS[h, ql, 2, :], NK[h, ql, 64:128], False, True, (K("TPS", hi), K("NK", hi)), (ypn,))
            for hi, h in enumerate(hs):
                cp[(hi + sub) % 2](YG[h, qsl, cs], yps[hi][0][h, 0:256].rearrange("p (a b) -> p a b", a=4), (yps[hi][1],), (K("YG", hi),))
        sps = [nb() for _ in hs]
        for i4, ql in enumerate(qls):
            for hi, h in enumerate(hs):
                sp_, spn = sps[hi]
                o = slice(i4 * 64, (i4 + 1) * 64)
                mm(sp_[h, o], TPS[h, ql, 0, :], UT[h, ql, :], True, False, (K("TPS", hi), utk(hi)), (spn,))
                mm(sp_[h, o], TPS[h, ql, 1, :], TPS[h, ql, 2, :], False, True, (K("TPS", hi),), (spn,))
        for hi, h in enumerate(hs):
            sp_, spn = sps[hi]
            for i4, ql in enumerate(qls):
                q = g * GP + ql
                stt(S32[h, q, :], S32[h, q, :], GC[h, ql, c:c + 1], sp_[h, i4 * 64:(i4 + 1) * 64], ALU.mult, ALU.add,
                    (K("S32", hi), f"GC{ql}", spn), (K("S32", hi),))
            q0 = g * GP + 4 * sub
            acopy(SBF[h, q0:q0 + 4, :], S32[h, q0:q0 + 4, :], (K("S32", hi),), (K("SBF", hi),))
        yield

    def post_pair(q, ql):
        par = lambda j: PA[:, q, j:j + 1]
        Y = YG[:, ql, :]
        acopy(YB[:], Y, ("YG00", "YG01", "YG10", "YG11"), ("YB",))
        act(Y2B[:], Y, AF.Square, ("YG00", "YG01", "YG10", "YG11"), ("Y2B",))
        pm, pmn = nb()
        mm(pm[:, 0:TT], BONES, YB[:], True, True, ("CST", "YB"), (pmn,))
        pe, pen = nb()
        mm(pe[:, 0:TT], BONES, Y2B[:], True, True, ("CST", "Y2B"), (pen,))
        act(M2[:], pm[:, 0:TT], AF.Square, (pmn,), ("M2",))
        tt(VAR[:], pe[:, 0:TT], M2[:], ALU.subtract, (pen, "M2"), ("VAR",))
        ts(VAR[:], VAR[:], 0.0, GN_EPS, ALU.max, ALU.add, ("VAR",), ("VAR",))
        act(SD[:], VAR[:], AF.Sqrt, ("VAR",), ("SD",))
        recip(RS[:], SD[:], ("SD",), ("RS",))
        tt(TQ[:], Y, pm[:, 0:TT], ALU.subtract, ("YG00", "YG01", "YG10", "YG11", pmn), ("TQ",))
        yield
        ptt(TQ[:], TQ[:], RS[:], ALU.mult, ("TQ", "RS"), ("TQ",))
        pts(TQ[:], TQ[:], par(8), par(9), ALU.mult, ALU.add, ("TQ", "PA"), ("TQ",))
        yield
        ptt(TQ[:], TQ[:], BON[:, ql, :], ALU.add, ("TQ", f"BON{ql}"), ("TQ",))
        ptt(TQ[:], TQ[:], GG[:, ql, :], ALU.mult, ("TQ", f"GG{ql}"), ("TQ",))
        yield
        bk, bkk = gemm16(T_CONV(q, 1), xb_rhs, ("XB",))
        acopy(CCS[:], bk[:, 0:TT], (bkk,), ("CCS",))
        yield
        bk, bkk = gemm16(T_CONV(q, 2), xb_rhs, ("XB",))
        pcopy(UC[:, 0:2], UCR[:, q, :], (f"UCR{q}",), ("UC",))
        tt(UC[:, 2:TT + 2], CCS[:], bk[:, 0:TT], ALU.mult, ("CCS", bkk), ("UC",))
        pcopy(UCR[:, q, :], UC[:, TT:TT + 2], ("UC",), (f"UCR{q}",))
        yield
        pts(ACC[:], UC[:, 0:TT], par(10), 1.0, ALU.mult, ALU.mult, ("UC", "PA"), ("ACC",))
        stt(ACC[:], UC[:, 1:TT + 1], par(11), ACC[:], ALU.mult, ALU.add, ("UC", "PA", "ACC"), ("ACC",))
        stt(ACC[:], UC[:, 2:TT + 2], par(12), ACC[:], ALU.mult, ALU.add, ("UC", "PA", "ACC"), ("ACC",))
        yield
        bk, bkk = gemm16(T_CONV(q, 0), xb_rhs, ("XB",))
        tt(YC[:], bk[:, 0:TT], ACC[:], ALU.mult, (bkk, "ACC"), ("YC",))
        yield
        bk, bkk = gemm16(T_CONV(q, 3), xb_rhs, ("XB",))
        act(SGC[:], bk[:, 0:TT], AF.Sigmoid, (bkk,), ("SGC",))
        yield
        ptt(YC[:], YC[:], SGC[:], ALU.mult, ("YC", "SGC"), ("YC",))
        bk, bkk = gemm16(T_CONV(q, 4), xb_rhs, ("XB",))
        act(SGR[:], bk[:, 0:TT], AF.Sigmoid, (bkk,), ("SGR",))
        yield
        ptt(TQ[:], TQ[:], SGR[:], ALU.mult, ("TQ", "SGR"), ("TQ",))
        ptt(MT[:, q, :], TQ[:], YC[:], ALU.add, ("TQ", "YC"), (f"MT{q}",))
        yield

    def layer_norm(gi, bi, want_bf):
        pm, pmn = nb()
        pe, pen = nb()
        for m in range(NKC):
            acopy(HB[:, m % 2, :], R[:, m, :], (f"R{m}",), (f"HB{m % 2}",))
            act(HS[:, m % 2, :], R[:, m, :], AF.Square, (f"R{m}",), (f"HS{m % 2}",))
            mm(pm[:, 0:TT], LONES, HB[:, m % 2, :], m == 0, m == NKC - 1, ("CST", f"HB{m % 2}"), (pmn,))
            mm(pe[:, 0:TT], LONES, HS[:, m % 2, :], m == 0, m == NKC - 1, ("CST", f"HS{m % 2}"), (pen,))
        act(M2[:], pm[:, 0:TT], AF.Square, (pmn,), ("M2",))
        acopy(MEAN[:], pm[:, 0:TT], (pmn,), ("MEAN",))
        tt(VAR[:], pe[:, 0:TT], M2[:], ALU.subtract, (pen, "M2"), ("VAR",))
        ts(VAR[:], VAR[:], 0.0, LN_EPS, ALU.max, ALU.add, ("VAR",), ("VAR",))
        act(SD[:], VAR[:], AF.Sqrt, ("VAR",), ("SD",))
        recip(RS[:], SD[:], ("SD",), ("RS",))
        for m in range(NKC):
            (tt if m % 2 == 0 else ptt)(R[:, m, :], R[:, m, :], MEAN[:], ALU.subtract, (f"R{m}", "MEAN"), (f"R{m}",))
            (tt if m % 2 == 0 else ptt)(R[:, m, :], R[:, m, :], RS[:], ALU.mult, (f"R{m}", "RS"), (f"R{m}",))
            (ts if m % 2 == 0 else pts)(R[:, m, :], R[:, m, :], PLN[:, gi, m:m + 1], PLN[:, bi, m:m + 1], ALU.mult, ALU.add, (f"R{m}", "PLN"), (f"R{m}",))
            if want_bf:
                acopy(X1B[:, m, :], R[:, m, :], (f"R{m}",), ("X1B",))

    def ffn_tile(i):
        mt_rhs = lambda kc: MT[:, kc, :]
        for m in range(NKC):
            bk, bkk = gemm16(T_WO(m), mt_rhs, tuple(f"MT{q}" for q in range(NPAIR)))
            stt(R[:, m, :], R[:, m, :], ALPHA, bk[:, 0:TT], ALU.mult, ALU.add, (f"R{m}", bkk), (f"R{m}",))
        layer_norm(0, 1, True)
        x1_rhs = lambda kc: X1B[:, kc, :]
        for f in range(NFC):
            bg, bgn = gemm16(T_GU(f, 0), x1_rhs, ("X1B",))
            bu, bun = gemm16(T_GU(f, 1), x1_rhs, ("X1B",))
            sg, sgn = SG[f % 2], f"SG{f % 2}"
            act(sg[:], bg[:, 0:TT], AF.Silu, (bgn,), (sgn,))
            tt(ACTB[:, f, :], sg[:], bu[:, 0:TT], ALU.mult, (sgn, bun), (f"ACTB{f}",) + OAK)
        for m in range(NKC):
            bk, bkk = nb()
            for qd in range(4):
                wi = (m * 4 + qd) % 3
                dma('sp', WDB[wi][:].rearrange("p a b -> p (a b)"), wdb[m][:, qd * 11 * 128:(qd + 1) * 11 * 128],
                    (f"wdb{m}",), (f"WDB{wi}",))
                for f11 in range(11):
                    f = qd * 11 + f11
                    mm(bk[:, 0:TT], WDB[wi][:, f11, :], ACTB[:, f, :], f == 0, f == NFC - 1, (f"WDB{wi}", f"ACTB{f}") + OAK, (bkk,))
            stt(R[:, m, :], R[:, m, :], ALPHA, bk[:, 0:TT], ALU.mult, ALU.add, (f"R{m}", bkk), (f"R{m}",))
        layer_norm(2, 3, False)
        dma('sp', outT[i], R[:].rearrange("p a b -> p (a b)"), RK, ("outT",))

    for i in range(NT):
        do_tile(xp, i, False, last_pre=(i == NT - 1))
    for i in range(NT):
        do_tile(xo, i, True)

    P.emit(nc, es)
    es.close()
    nc.all_engine_barrier()
    for sm in P.all_sems:
        nc.gpsimd.sem_clear(sm)
    nc.all_engine_barrier()
    return nc


def _wtile(w, cols):
    t = np.zeros((128, NKC, 128), np.float32)
    sub = w[:, cols]
    t[:, :, :sub.shape[1]] = sub.reshape(NKC, 128, -1).transpose(1, 0, 2)
    return t.reshape(128, NKC * 128)


def _host_weights(inp):
    w_in = np.asarray(inp["w_in"][0], np.float32)
    w_o = np.asarray(inp["w_o"][0], np.float32)
    w_gu = np.asarray(inp["w_gu"][0], np.float32)
    w_down = np.asarray(inp["w_down"][0], np.float32)
    wa = np.zeros((N_WA, 128, 2048), np.float32)
    lo = RWKV_LO + 3 * D
    wa[T_WD] = _wtile(w_in, np.arange(lo, lo + 96))
    wa[T_AD] = _wtile(w_in, np.arange(lo + 96, lo + 192))
    wa[T_GD0] = _wtile(w_in, np.arange(lo + 192, lo + 320))
    wa[T_GD1] = _wtile(w_in, np.arange(lo + 320, lo + 448))
    for q in range(NPAIR):
        ch = np.arange(128 * q, 128 * q + 128)
        for j in range(3):
            wa[T_RKV(q, j)] = _wtile(w_in, RWKV_LO + j * D + ch)
        wa[T_CONV(q, 0)] = _wtile(w_in, ch)
        wa[T_CONV(q, 1)] = _wtile(w_in, D + ch)
        wa[T_CONV(q, 2)] = _wtile(w_in, 2 * D + ch)
        wa[T_CONV(q, 3)] = _wtile(w_in, RWKV_HI + ch)
        wa[T_CONV(q, 4)] = _wtile(w_in, RWKV_HI + D + ch)
    for m in range(16):
        wa[T_WO(m)] = _wtile(w_o, np.arange(128 * m, 128 * m + 128))
    for f in range(NFC):
        wa[T_GU(f, 0)] = _wtile(w_gu, np.arange(128 * f, 128 * f + 128))
        wa[T_GU(f, 1)] = _wtile(w_gu, DFF + np.arange(128 * f, 128 * f + 128))
    wdn = np.ascontiguousarray(
        w_down.reshape(NFC, 128, 16, 128).transpose(2, 1, 0, 3)).reshape(16, 128, DFF)
    mu = np.asarray(inp["shift_mu"][0], np.float32)
    pa = np.zeros((128, NPAIR, NPAR), np.float32)
    def pc(v):
        return np.asarray(v, np.float32).reshape(NPAIR, 128).T
    pa[:, :, 0] = pc(mu[0:D]); pa[:, :, 1] = pc(mu[D:2 * D]); pa[:, :, 2] = pc(mu[2 * D:3 * D])
    pa[:, :, 3] = pc(inp["w0"][0]); pa[:, :, 4] = pc(inp["a0"][0]); pa[:, :, 5] = pc(inp["k_k"][0])
    pa[:, :, 6] = pc(inp["k_a"][0]); pa[:, :, 7] = pc(np.asarray(inp["r_k"][0]).reshape(-1))
    pa[:, :, 8] = pc(inp["gn_g"][0]); pa[:, :, 9] = pc(inp["gn_b"][0])
    cw = np.asarray(inp["conv_w"][0], np.float32)
    pa[:, :, 10] = pc(cw[0]); pa[:, :, 11] = pc(cw[1]); pa[:, :, 12] = pc(cw[2])
    pl = np.zeros((128, 4), np.float32)
    pl[:96, 0] = mu[3 * D:3 * D + 96]; pl[:96, 1] = mu[3 * D + 96:3 * D + 192]
    pl[:, 2] = mu[3 * D + 192:3 * D + 320]; pl[:, 3] = mu[3 * D + 320:3 * D + 448]
    pln = np.zeros((128, 4, 16), np.float32)
    for j, k in enumerate(("ln1_g", "ln1_b", "ln2_g", "ln2_b")):
        pln[:, j, :] = np.asarray(inp[k][0], np.float32).reshape(16, 128).T
    lup = np.zeros((128, 4, NPAIR, 128), np.float32)
    lup[:96, 0] = np.asarray(inp["w_up"][0], np.float32).reshape(96, NPAIR, 128)
    lup[:96, 1] = np.asarray(inp["a_up"][0], np.float32).reshape(96, NPAIR, 128)
    gu = np.asarray(inp["g_up"][0], np.float32).reshape(2, 128, NPAIR, 128)
    lup[:, 2] = gu[0]; lup[:, 3] = gu[1]
    cst = np.zeros((128, 128 * 3 + 512 + 512 + TT + 256), np.float32)
    cst[:, 0:128] = np.eye(128)
    blk = np.zeros((128, 128)); blk[:64, :64] = 1.0 / 64; blk[64:, 64:] = 1.0 / 64
    cst[:, 128:256] = blk
    cst[:, 256:384] = 1.0 / 2048
    s = (np.arange(128) % 64)[:, None]; t = np.arange(64)[None, :]
    mp = np.concatenate([(s < t), (s <= t)], 1).astype(np.float32)
    cst[:, 384:896] = np.tile(mp, (1, 4))
    cst[:, 896:1408] = np.tile((t < s).astype(np.float32), (1, 8))
    rs_ = np.ones(TT, np.float32); rs_[::64] = 0.0
    cst[:, 1408:1408 + TT] = rs_[None, :]
    cst[:, 1408 + TT:1408 + TT + 256] = np.tile(np.eye(64, dtype=np.float32)[np.arange(128) % 64], (1, 4))
    return dict(wa=wa, wdn=wdn, pa=pa.reshape(128, -1), pl=pl, pln=pln.reshape(128, -1),
                lup=lup.reshape(128, -1), cst=cst.astype(ml_dtypes.bfloat16))


def _xtiles(xs):
    T = xs.shape[0]
    return np.ascontiguousarray(xs.reshape(T // TT, TT, NKC, 128).transpose(0, 3, 2, 1)).reshape(T // TT, 128, NKC * TT)


def kernel(**inputs):
    x = np.asarray(inputs["x"], np.float32)
    B, T, _ = x.shape
    half = T // 2
    NT = half // TT
    shared = _host_weights(inputs)
    nc = build_program(NT)
    in_maps = []
    for c in range(8):
        b, sh = c // 2, c % 2
        xo = _xtiles(x[b, sh * half:(sh + 1) * half])
        xp = _xtiles(x[b, 0:half]) if sh == 1 else np.zeros_like(xo)
        m = dict(shared)
        m["xo"] = xo
        m["xp"] = xp
        in_maps.append(m)
    res = run_bass_kernel_spmd(nc, in_maps, core_ids=list(range(8)))
    out = np.empty((B, T, D), np.float32)
    for c in range(8):
        b, sh = c // 2, c % 2
        o = np.asarray(res.results[c]["outT"]).reshape(NT, 128, NKC, TT)
        out[b, sh * half:(sh + 1) * half] = o.transpose(0, 3, 2, 1).reshape(half, D)
    return out
```

```python
import math
from contextlib import ExitStack
import numpy as np
import ml_dtypes
import concourse.bass as bass
import concourse.mybir as mybir
from concourse.bass_utils import run_bass_kernel_spmd

F32, BF16 = mybir.dt.float32, mybir.dt.bfloat16
AF, ALU = mybir.ActivationFunctionType, mybir.AluOpType

D = 2048
TT = 256
NCH = TT // 64
NKC = 16
NPAIR = 16
GP = 8
DFF = 5632
NFC = 44
SEQ = 8192
ALPHA = 2.0 ** 0.25
LN_EPS = 1e-5
GN_EPS = 64e-5
CNEG = -math.exp(-0.5)
RWKV_LO = 3 * D
RWKV_HI = RWKV_LO + 3 * D + 96 + 96 + 256
T_WD, T_AD, T_GD0, T_GD1 = 0, 1, 2, 3
def T_RKV(q, j): return 4 + 3 * q + j
def T_CONV(q, j): return 52 + 5 * q + j
def T_WO(m): return 132 + m
def T_GU(f, j): return 148 + 2 * f + j
N_WA = 236
NPAR = 16


class Sched:
    def __init__(self):
        self.ops = []

    KEYMAP = {"M2": "E1", "VAR": "E2", "SD": "E3", "RS": "E4", "TQ": "GX", "CCS": "ZV", "ACC": "LW",
              "YC": "AL", "SGC": "SN", "SGR": "RN", "MEAN": "KKN", "SG0": "T1", "SG1": "KM", "X1B": "XB"}
    LANEKEYS = {"RW0", "RW1", "RW2", "DT", "ZR", "ZK", "ZV", "LW", "GS", "GX", "E1", "E2", "E3", "E4", "AL", "SN",
                "RN", "KKN", "T1", "KM", "KA", "SQB", "EB", "YB", "Y2B", "UC"}
    lane = 0

    def _key(self, k):
        k = self.KEYMAP.get(k, k)
        return k + f"@{self.lane}" if k in self.LANEKEYS else k

    def add(self, eng, fn, reads, writes, dma=False):
        self.ops.append(dict(eng=eng, fn=fn, reads=tuple(self._key(k) for k in reads),
                             writes=tuple(self._key(k) for k in writes), dma=dma))

    def emit(self, nc, es):
        ops = self.ops
        engs = ['pe', 'act', 'dve', 'pool', 'sp']
        last_w, readers = {}, {}
        waited = {e: {f: -1 for f in engs} for e in engs}
        waited_dma = {e: set() for e in engs}
        for i, op in enumerate(ops):
            deps = set()
            for k in op['reads']:
                if k in last_w:
                    deps.add(last_w[k])
            for k in op['writes']:
                if k in last_w:
                    deps.add(last_w[k])
                deps.update(readers.get(k, ()))
            for k in op['reads']:
                readers.setdefault(k, []).append(i)
            for k in op['writes']:
                last_w[k] = i
                readers[k] = []
            deps.discard(i)
            e = op['eng']
            w_eng, w_dma = {}, []
            for j in deps:
                oj = ops[j]
                if oj['dma']:
                    if j not in waited_dma[e]:
                        waited_dma[e].add(j)
                        w_dma.append(j)
                else:
                    f = oj['eng']
                    if f == e and e == 'pe':
                        continue
                    if j > waited[e][f]:
                        w_eng[f] = max(w_eng.get(f, -1), j)
            for f, j in w_eng.items():
                waited[e][f] = j
                ops[j]['sig'] = True
            op['w_eng'] = w_eng
            op['w_dma'] = w_dma
        cnt = {e: 0 for e in engs}
        for op in ops:
            if op.get('sig') and not op['dma']:
                cnt[op['eng']] += 1
                op['signum'] = cnt[op['eng']]
        sig_sem = {e: es.enter_context(nc.semaphore("sg_" + e)) for e in ['pe', 'act', 'dve', 'pool']}
        NDS = 24
        dsem = {q: [es.enter_context(nc.semaphore(f"d_{q}{i}")) for i in range(NDS)] for q in ['sp', 'pool']}
        dcount = {'sp': 0, 'pool': 0}
        for op in ops:
            if op['dma']:
                q = op['eng']
                k = dcount[q]
                dcount[q] += 1
                op['dsem'] = dsem[q][k % NDS]
                op['dval'] = 16 * (k // NDS + 1)
        per_eng = {e: [op for op in ops if op['eng'] == e] for e in engs}
        all_sems = list(sig_sem.values()) + dsem['sp'] + dsem['pool']
        nc.all_engine_barrier()
        for sm in all_sems:
            nc.gpsimd.sem_clear(sm)
        nc.all_engine_barrier()
        self.all_sems = all_sems
        block = es.enter_context(nc.Block())

        def run(eng_handle, ename):
            for op in per_eng[ename]:
                for f, j in op['w_eng'].items():
                    eng_handle.wait_ge(sig_sem[f], ops[j]['signum'])
                for j in op['w_dma']:
                    eng_handle.wait_ge(ops[j]['dsem'], ops[j]['dval'])
                if op['dma'] and op['dval'] > 16:
                    eng_handle.wait_ge(op['dsem'], op['dval'] - 16)
                inst = op['fn'](eng_handle)
                if op['dma']:
                    inst.then_inc(op['dsem'], 16)
                elif op.get('sig'):
                    inst.then_inc(sig_sem[ename], 1)
            if ename in ('sp', 'pool'):
                k = dcount[ename]
                for i in range(min(NDS, k)):
                    n_i = (k - 1 - i) // NDS + 1
                    eng_handle.wait_ge(dsem[ename][i], 16 * n_i)

        @block.tensor
        def _(e):
            run(e, 'pe')

        @block.scalar
        def _(e):
            run(e, 'act')

        @block.vector
        def _(e):
            run(e, 'dve')

        @block.gpsimd
        def _(e):
            run(e, 'pool')

        @block.sync
        def _(e):
            run(e, 'sp')


def build_program(NT):
    nc = bass.Bass("TRN2", target_bir_lowering=False)
    es = ExitStack()
    P = Sched()

    def dram(name, shape, dt, kind):
        return nc.dram_tensor(name, shape, dt, kind=kind).ap()

    xo = dram("xo", [NT, 128, NKC * TT], F32, "ExternalInput")
    xp = dram("xp", [NT, 128, NKC * TT], F32, "ExternalInput")
    wa = dram("wa", [N_WA, 128, 2048], F32, "ExternalInput")
    wdn = dram("wdn", [16, 128, DFF], F32, "ExternalInput")
    pa_d = dram("pa", [128, NPAIR * NPAR], F32, "ExternalInput")
    pl_d = dram("pl", [128, 4], F32, "ExternalInput")
    pln_d = dram("pln", [128, 64], F32, "ExternalInput")
    lup_d = dram("lup", [128, 4 * NPAIR * 128], F32, "ExternalInput")
    cst_d = dram("cst", [128, 128 * 3 + 512 + 512 + TT + 256], BF16, "ExternalInput")
    outT = dram("outT", [NT, 128, NKC * TT], F32, "ExternalOutput")
    wab = dram("wab", [N_WA, 128, 2048], BF16, "Internal")
    wdb = dram("wdb", [16, 128, DFF], BF16, "Internal")

    def sb(name, shape, dt):
        return es.enter_context(nc.sbuf_tensor(name, shape, dt))

    def ps(name, shape, dt):
        return es.enter_context(nc.psum_tensor(name, shape, dt))

    PA = sb("PA", [128, NPAIR, NPAR], F32)
    PL = sb("PL", [128, 4], F32)
    PLN = sb("PLN", [128, 4, 16], F32)
    LUP = sb("LUP", [128, 4, NPAIR, 128], BF16)
    CST = sb("CST", [128, 128 * 3 + 512 + 512 + TT + 256], BF16)
    IDENT = CST[:, 0:128]
    BONES = CST[:, 128:256]
    LONES = CST[:, 256:384]
    MASKP = CST[:, 384:896]
    MASKT = CST[:, 896:1408]
    RESETb = CST[:, 1408:1408 + TT]
    IDENT4 = CST[:, 1408 + TT:1408 + TT + 256]

    RESET = sb("RESET", [128, TT], F32)
    R = sb("R", [128, NKC, TT], F32)
    RK = tuple(f"R{m}" for m in range(NKC))
    XB = sb("XB", [128, NKC, TT], BF16)
    NWB = 4
    WB = [sb(f"WB{i}", [128, NKC, 128], BF16) for i in range(NWB)]
    WDB = [sb(f"WDB{i}", [128, 11, 128], BF16) for i in range(3)]
    CR = sb("CR", [128, NPAIR * 3 + 4], F32)
    UCR = sb("UCR", [128, NPAIR, 2], F32)
    S32 = sb("S32", [128, NPAIR, 64], F32)
    SBF = sb("SBF", [128, NPAIR, 64], BF16)
    GC = sb("GC", [128, GP, NCH], F32)
    RWL = sb("RWL", [128, TT + 1], F32)
    TWD = sb("TWD", [128, TT], BF16)
    ZAD = sb("ZAD", [128, TT], BF16)
    SGD = sb("SGD", [128, 2, TT], BF16)
    class LT:
        def __init__(self, ts_):
            self.ts_ = ts_
        def __getitem__(self, idx):
            return self.ts_[P.lane][idx]

    def t32(name, n=TT):
        return LT([sb(name, [128, n], F32), sb(name + "_b", [128, n], F32)])
    RW = [t32(f"RW{j}", TT + 1) for j in range(3)]
    DT_ = t32("DT")
    ZR, ZK, ZV = t32("ZR"), t32("ZK"), t32("ZV")
    LW, GS, GX = t32("LW"), t32("GS"), t32("GX")
    E1, E2, E3, E4 = t32("E1"), t32("E2"), t32("E3"), t32("E4")
    AL, SN, RN, KKN, T1, KM, KA = t32("AL"), t32("SN"), t32("RN"), t32("KKN"), t32("T1"), t32("KM"), t32("KA")
    SQB = LT([sb("SQB", [128, TT], BF16), sb("SQB_b", [128, TT], BF16)])
    EB = LT([sb("EB", [128, TT], BF16), sb("EB_b", [128, TT], BF16)])
    ARENA = sb("ARENA", [128, 7 * GP * TT], BF16)
    AR = ARENA[:, 0:2 * GP * TT].rearrange("p (a b c) -> p a b c", a=GP, b=NCH)
    def _ar(j):
        return ARENA[:, (2 + j) * GP * TT:(3 + j) * GP * TT].rearrange("p (a b) -> p a b", a=GP)
    BT, KT, BH, KH, VB = (_ar(j) for j in range(5))
    BON = sb("BON", [128, GP, TT], BF16)
    GG = sb("GG", [128, GP, TT], F32)
    YG = sb("YG", [128, GP, TT], F32)
    TPS = sb("TPS", [128, GP, 3, 64], BF16)
    NB = sb("NB", [128, GP, 128], BF16)
    NK = sb("NK", [128, GP, 128], BF16)
    NT0 = sb("NT0", [128, GP, 64], BF16)
    NN = [sb(f"NN{j}", [128, GP, 128], BF16) for j in range(5)]
    XA = [sb(f"XA{j}", [128, GP, 64], BF16) for j in range(2)]
    YB = LT([sb("YB", [128, TT], BF16), sb("YB_b", [128, TT], BF16)])
    Y2B = LT([sb("Y2B", [128, TT], BF16), sb("Y2B_b", [128, TT], BF16)])
    M2, VAR, SD, RS, TQ = E1, E2, E3, E4, GX
    CCS = ZV
    UC = t32("UC", TT + 2)
    ACC, YC, SGC, SGR = LW, AL, SN, RN
    MT = sb("MT", [128, NPAIR, TT], BF16)
    HB = sb("HB", [128, 2, TT], BF16)
    HS = sb("HS", [128, 2, TT], BF16)
    MEAN = KKN
    X1B = XB
    SG = [T1, KM]
    ACTB = ARENA[:, 0:NFC * TT].rearrange("p (a b) -> p a b", a=NFC)
    OAK = tuple(f"OA{ql}" for ql in range(GP))
    NBK = 8
    BK = [ps(f"BK{i}", [128, 512], F32) for i in range(NBK)]
    bk_i = [0]

    def nb():
        i = bk_i[0] % NBK
        bk_i[0] += 1
        return BK[i], f"BK{i}"

    def dma(q, out, in_, reads, writes, **kw):
        P.add(q, lambda e: e.dma_start(out=out, in_=in_, **kw), reads, writes, dma=True)

    def mm(out, lhsT, rhs, start, stop, reads, writes):
        P.add('pe', lambda e: e.matmul(out, lhsT=lhsT, rhs=rhs, start=start, stop=stop), reads, writes)

    def tr(out, in_, ident, reads, writes):
        P.add('pe', lambda e: e.transpose(out, in_, ident), reads, writes)

    def act(out, in_, func, reads, writes, bias=None, scale=None):
        kw = {}
        if bias is not None:
            kw['bias'] = bias
        if scale is not None:
            kw['scale'] = scale
        P.add('act', lambda e: e.activation(out=out, in_=in_, func=func, **kw), reads, writes)

    def acopy(out, in_, reads, writes):
        P.add('act', lambda e: e.copy(out=out, in_=in_), reads, writes)

    def vcopy(out, in_, reads, writes):
        P.add('dve', lambda e: e.tensor_copy(out=out, in_=in_), reads, writes)

    def tt(out, in0, in1, op, reads, writes):
        P.add('dve', lambda e: e.tensor_tensor(out=out, in0=in0, in1=in1, op=op), reads, writes)

    def ts(out, in0, s1, s2, op0, op1, reads, writes):
        if op1 is None:
            P.add('dve', lambda e: e.tensor_scalar(out=out, in0=in0, scalar1=s1, scalar2=None, op0=op0), reads, writes)
        else:
            P.add('dve', lambda e: e.tensor_scalar(out=out, in0=in0, scalar1=s1, scalar2=s2, op0=op0, op1=op1), reads, writes)

    def stt(out, in0, scalar, in1, op0, op1, reads, writes):
        P.add('dve', lambda e: e.scalar_tensor_tensor(out=out, in0=in0, scalar=scalar, in1=in1, op0=op0, op1=op1), reads, writes)

    def ptt(out, in0, in1, op, reads, writes):
        P.add('pool', lambda e: e.tensor_tensor(out=out, in0=in0, in1=in1, op=op), reads, writes)

    def pts(out, in0, s1, s2, op0, op1, reads, writes):
        P.add('pool', lambda e: e.tensor_scalar(out=out, in0=in0, scalar1=s1, scalar2=s2, op0=op0, op1=op1), reads, writes)

    def pcopy(out, in_, reads, writes):
        P.add('pool', lambda e: e.tensor_copy(out=out, in_=in_), reads, writes)

    def recip(out, in_, reads, writes):
        P.add('dve', lambda e: e.reciprocal(out=out, in_=in_), reads, writes)

    def memset(ap, val, writes):
        P.add('dve', lambda e: e.memset(ap, val), (), writes)

    def conv_w(t):
        dma('pool', wab[t], wa[t], (), (f"wab{t}",), max_dma_last_dim=8192)

    def conv_wd(m):
        dma('pool', wdb[m], wdn[m], (), (f"wdb{m}",), max_dma_last_dim=8192)

    order = [T_WD, T_AD]
    for q in range(NPAIR):
        order += [T_RKV(q, 1), T_RKV(q, 2)]
    order += [T_GD0, T_GD1]
    for g in range(NPAIR // GP):
        for q in range(g * GP, (g + 1) * GP):
            order.append(T_RKV(q, 0))
        for q in range(g * GP, (g + 1) * GP):
            order += [T_CONV(q, j) for j in (1, 2, 0, 3, 4)]
    order += [T_WO(m) for m in range(16)]
    for f in range(NFC):
        order += [T_GU(f, 0), T_GU(f, 1)]
    assert sorted(order) == list(range(N_WA))
    for t in order:
        conv_w(t)
    for m in range(16):
        conv_wd(m)

    dma('sp', PA[:].rearrange("p a b -> p (a b)"), pa_d, (), ("PA",))
    dma('sp', PL[:], pl_d, (), ("PL",))
    dma('sp', PLN[:].rearrange("p a b -> p (a b)"), pln_d, (), ("PLN",))
    dma('sp', CST[:], cst_d, (), ("CST",))
    for j in range(4):
        Rf = R[:].rearrange("p a b -> p (a b)")[:, 0:NPAIR * 128]
        dma('sp', Rf, lup_d[:, j * NPAIR * 128:(j + 1) * NPAIR * 128], (), RK)
        acopy(LUP[:, j].rearrange("p a b -> p (a b)"), Rf, RK, ("LUP",))
    vcopy(RESET[:], RESETb, ("CST",), ("RESET",))
    ts(PA[:, :, 13], PA[:, :, 6], -1.0, 1.0, ALU.mult, ALU.add, ("PA",), ("PA",))
    memset(CR[:], 0.0, tuple(f"CR{j}" for j in range(NPAIR * 3 + 4)))
    memset(UCR[:].rearrange("p a b -> p (a b)"), 0.0, tuple(f"UCR{q}" for q in range(NPAIR)))
    memset(S32[:].rearrange("p a b -> p (a b)"), 0.0, ("S3200", "S3201", "S3210", "S3211"))
    memset(SBF[:].rearrange("p a b -> p (a b)"), 0.0, ("SBF00", "SBF01", "SBF10", "SBF11"))

    wb_i = [0]

    def load_w(t):
        i = wb_i[0] % NWB
        wb_i[0] += 1
        dma('sp', WB[i][:].rearrange("p a b -> p (a b)"), wab[t], (f"wab{t}",), (f"WB{i}",))
        return WB[i], f"WB{i}"

    def gemm16(t, rhs_fn, rhs_keys, M=128):
        w, wk = load_w(t)
        bk, bkk = nb()
        for kc in range(NKC):
            mm(bk[:M, 0:TT], w[:, kc, 0:M], rhs_fn(kc), kc == 0, kc == NKC - 1, (wk,) + tuple(rhs_keys), (bkk,))
        return bk, bkk

    xb_rhs = lambda kc: XB[:, kc, :]

    def shift(bk, bkk, rw, rwk, slot, mu_ap, z, zk_, M=128):
        acopy(rw[:M, 1:TT + 1], bk[:M, 0:TT], (bkk,), (rwk,))
        pcopy(rw[:M, 0:1], CR[:M, slot:slot + 1], (f"CR{slot}",), (rwk,))
        pcopy(CR[:M, slot:slot + 1], rw[:M, TT:TT + 1], (rwk,), (f"CR{slot}",))
        ptt(DT_[:M, :], rw[:M, 0:TT], rw[:M, 1:TT + 1], ALU.subtract, (rwk,), ("DT",))
        stt(z, DT_[:M, :], mu_ap, rw[:M, 1:TT + 1], ALU.mult, ALU.add, ("DT", rwk, "PA", "PL"), (zk_,))

    c4 = lambda ap: ap.rearrange("p (c t) -> p c t", c=NCH)

    def carries_only():
        for j in range(2):
            bk, bkk = gemm16(T_GD0 + j, xb_rhs, ("XB",))
            vcopy(CR[:, NPAIR * 3 + 2 + j:NPAIR * 3 + 3 + j], bk[:, TT - 1:TT], (bkk,), (f"CR{NPAIR * 3 + 2 + j}",))
        for q in range(NPAIR):
            bk, bkk = gemm16(T_RKV(q, 0), xb_rhs, ("XB",))
            vcopy(CR[:, q * 3:q * 3 + 1], bk[:, TT - 1:TT], (bkk,), (f"CR{q * 3}",))
            bk, bkk = gemm16(T_CONV(q, 1), xb_rhs, ("XB",))
            acopy(CCS[:, 0:2], bk[:, TT - 2:TT], (bkk,), ("CCS",))
            bk, bkk = gemm16(T_CONV(q, 2), xb_rhs, ("XB",))
            tt(UCR[:, q, :], CCS[:, 0:2], bk[:, TT - 2:TT], ALU.mult, ("CCS", bkk), (f"UCR{q}",))

    def do_tile(xsrc, i, own, last_pre=False):
        dma('sp', R[:].rearrange("p a b -> p (a b)"), xsrc[i], (), RK)
        for kc in range(NKC):
            if kc % 2 == 0:
                acopy(XB[:, kc, :], R[:, kc, :], (f"R{kc}",), ("XB",))
            else:
                vcopy(XB[:, kc, :], R[:, kc, :], (f"R{kc}",), ("XB",))
        bk, bkk = gemm16(T_WD, xb_rhs, ("XB",))
        shift(bk, bkk, RWL, "RWL", NPAIR * 3 + 0, PL[:, 0:1], DT_[:, :], "DT")
        act(TWD[:], DT_[:, :], AF.Tanh, ("DT",), ("TWD",))
        bk, bkk = gemm16(T_AD, xb_rhs, ("XB",))
        shift(bk, bkk, RWL, "RWL", NPAIR * 3 + 1, PL[:, 1:2], DT_[:, :], "DT")
        acopy(ZAD[:], DT_[:, :], ("DT",), ("ZAD",))
        if own:
            for j in range(2):
                bk, bkk = gemm16(T_GD0 + j, xb_rhs, ("XB",))
                shift(bk, bkk, RWL, "RWL", NPAIR * 3 + 2 + j, PL[:, 2 + j:3 + j], DT_[:, :], "DT")
                act(SGD[:, j, :], DT_[:, :], AF.Sigmoid, ("DT",), ("SGD",))
        def lockstep(items):
            items = list(items)
            while items:
                for it in list(items):
                    P.lane = it[0]
                    try:
                        next(it[1])
                    except StopIteration:
                        items.remove(it)
            P.lane = 0

        for g in range(NPAIR // GP):
            for ql in range(0, GP, 2):
                lockstep([(ln, prep_pair(g * GP + ql + ln, ql + ln, own)) for ln in range(2)])
            for c in range(NCH):
                scan_chunk(g, c, own)
            if own:
                for ql in range(0, GP, 2):
                    lockstep([(ln, post_pair(g * GP + ql + ln, ql + ln)) for ln in range(2)])
        if own:
            ffn_tile(i)
        if last_pre:
            carries_only()

    def prep_pair(q, ql, own):
        par = lambda j: PA[:, q, j:j + 1]
        oak = f"OA{ql}"
        if own:
            bk, bkk = gemm16(T_RKV(q, 0), xb_rhs, ("XB",))
            shift(bk, bkk, RW[0], "RW0", q * 3 + 0, par(0), ZR[:], "ZR")
            yield
        bk, bkk = gemm16(T_RKV(q, 1), xb_rhs, ("XB",))
        shift(bk, bkk, RW[1], "RW1", q * 3 + 1, par(1), ZK[:], "ZK")
        yield
        bk, bkk = gemm16(T_RKV(q, 2), xb_rhs, ("XB",))
        shift(bk, bkk, RW[2], "RW2", q * 3 + 2, par(2), ZV[:], "ZV")
        yield
        bk, bkk = nb()
        mm(bk[:, 0:TT], LUP[:, 0, q, :], TWD[:], True, True, ("LUP", "TWD"), (bkk,))
        act(LW[:], bk[:, 0:TT], AF.Sigmoid, (bkk, "PA"), ("LW",), bias=par(3))
        yield
        gs_, lw_ = GS[:], LW[:]
        P.add('dve', lambda e: e.tensor_tensor_scan(out=gs_, data0=RESET[:], data1=lw_, initial=0.0,
                                                     op0=ALU.mult, op1=ALU.add), ("RESET", "LW"), ("GS",))
        yield
        act(E1[:], GS[:], AF.Exp, ("GS",), ("E1",), scale=CNEG)
        act(E2[:], GS[:], AF.Exp, ("GS",), ("E2",), scale=-CNEG)
        yield
        ptt(GX[:], GS[:], LW[:], ALU.subtract, ("GS", "LW"), ("GX",))
        act(E3[:], GX[:], AF.Exp, ("GX",), ("E3",), scale=CNEG)
        yield
        for c in range(NCH):
            pts(E4[:, c * 64:(c + 1) * 64], E2[:, c * 64:(c + 1) * 64], E1[:, c * 64 + 63:c * 64 + 64], 1.0,
                ALU.mult, ALU.mult, ("E1", "E2"), ("E4",))
        pcopy(GC[:, ql, :], c4(E1[:])[:, :, 63], ("E1",), (f"GC{ql}",))
        yield
        bk, bkk = nb()
        mm(bk[:, 0:TT], LUP[:, 1, q, :], ZAD[:], True, True, ("LUP", "ZAD"), (bkk,))
        act(AL[:], bk[:, 0:TT], AF.Sigmoid, (bkk, "PA"), ("AL",), bias=par(4))
        yield
        act(SQB[:], ZK[:], AF.Square, ("ZK", "PA"), ("SQB",), scale=par(5))
        bk, bkk = nb()
        mm(bk[:, 0:TT], BONES, SQB[:], True, True, ("CST", "SQB"), (bkk,))
        act(SN[:], bk[:, 0:TT], AF.Sqrt, (bkk,), ("SN",), scale=64.0, bias=1e-30)
        yield
        ts(SN[:], SN[:], 1e-12, None, ALU.max, None, ("SN",), ("SN",))
        recip(RN[:], SN[:], ("SN",), ("RN",))
        yield
        stt(KKN[:], ZK[:], par(5), RN[:], ALU.mult, ALU.mult, ("ZK", "PA", "RN"), ("KKN",))
        yield
        pts(T1[:], AL[:], par(6), par(13), ALU.mult, ALU.add, ("AL", "PA"), ("T1",))
        ptt(KM[:], ZK[:], T1[:], ALU.mult, ("ZK", "T1"), ("KM",))
        ptt(KA[:], KKN[:], AL[:], ALU.mult, ("KKN", "AL"), ("KA",))
        yield
        stt(AR[:, ql, :, 0:64], c4(KKN[:]), -1.0, c4(E3[:]), ALU.mult, ALU.mult, ("KKN", "E3"), (oak,))
        if own:
            ptt(AR[:, ql, :, 64:128], c4(ZR[:]), c4(E1[:]), ALU.mult, ("ZR", "E1"), (oak,))
        ptt(BT[:, ql, :], KA[:], E2[:], ALU.mult, ("KA", "E2"), (oak,))
        yield
        tt(KT[:, ql, :], KM[:], E2[:], ALU.mult, ("KM", "E2"), (oak,))
        ptt(BH[:, ql, :], KA[:], E4[:], ALU.mult, ("KA", "E4"), (oak,))
        yield
        tt(KH[:, ql, :], KM[:], E4[:], ALU.mult, ("KM", "E4"), (oak,))
        acopy(VB[:, ql, :], ZV[:], ("ZV",), (oak,))
        yield
        if own:
            stt(EB[:], ZR[:], par(7), KM[:], ALU.mult, ALU.mult, ("ZR", "PA", "KM"), ("EB",))
            bk, bkk = nb()
            mm(bk[:, 0:TT], BONES, EB[:], True, True, ("CST", "EB"), (bkk,))
            stt(BON[:, ql, :], bk[:, 0:TT], 64.0, ZV[:], ALU.mult, ALU.mult, (bkk, "ZV"), (f"BON{ql}",))
            yield
            bk, bkk = nb()
            for j in range(2):
                mm(bk[:, 0:TT], LUP[:, 2 + j, q, :], SGD[:, j, :], j == 0, j == 1, ("LUP", "SGD"), (bkk,))
            acopy(GG[:, ql, :], bk[:, 0:TT], (bkk,), (f"GG{ql}",))
            yield
        yield

    def scan_chunk(g, c, own):
        gens = [scan_gen(g, c, own, 0), scan_gen(g, c, own, 1)]
        while gens:
            for gn in list(gens):
                try:
                    next(gn)
                except StopIteration:
                    gens.remove(gn)

    def scan_gen(g, c, own, sub):
        cs = slice(c * 64, (c + 1) * 64)
        hs = [slice(0, 64), slice(64, 128)]
        qls = list(range(4 * sub, 4 * sub + 4))
        qsl = slice(4 * sub, 4 * sub + 4)
        cpn = [0]

        class _CP:
            def __getitem__(self, _):
                cpn[0] += 1
                return vcopy if cpn[0] % 4 == 0 else acopy
        cp = _CP()
        ncol = 128 if own else 64
        K = lambda base, hi: f"{base}{hi}{sub}"
        TPSf = TPS[:].rearrange("p a b c -> p (a b c)")
        for hi, h in enumerate(hs):
            b0, b0n = nb()
            v0 = b0[:, :].bitcast(BF16)
            for i4, ql in enumerate(qls):
                for j, src in enumerate((BH, KH, VB)):
                    o = (i4 * 3 + j) * 64
                    tr(v0[h, o:o + 64], src[h, ql, cs], IDENT[h, h], (f"OA{ql}", "CST"), (b0n,))
            cp[(hi + sub) % 2](TPSf[h, sub * 768:(sub + 1) * 768], v0[h, 0:768], (b0n,), (K("TPS", hi),))
        yield
        for src, dstN, dn in ((BT, NB, "NB"), (KT, NK, "NK")):
            banks = [nb() for _ in hs]
            for i4, ql in enumerate(qls):
                for hi, h in enumerate(hs):
                    b_, bn = banks[hi]
                    o = i4 * 128
                    mm(b_[h, o:o + ncol], src[h, ql, cs], AR[h, ql, c, 0:ncol], True, True, (f"OA{ql}",), (bn,))
            for hi, h in enumerate(hs):
                b_, bn = banks[hi]
                if own:
                    tt(dstN[h, qsl, :].rearrange("p a b -> p (a b)"), b_[h, :], MASKP[h, :], ALU.mult,
                       (bn, "CST"), (K(dn, hi),))
                else:
                    m3 = MASKP[h, :].rearrange("p (a b) -> p a b", a=4)[:, :, 0:64]
                    tt(dstN[h, qsl, 0:64], b_[h, :].rearrange("p (a b) -> p a b", a=4)[:, :, 0:64], m3,
                       ALU.mult, (bn, "CST"), (K(dn, hi),))
            yield
        pts = [nb() for _ in hs]
        for i4, ql in enumerate(qls):
            for hi, h in enumerate(hs):
                mm(pts[hi][0][h, i4 * 64:(i4 + 1) * 64], AR[h, ql, c, 0:64], BT[h, ql, cs], True, True, (f"OA{ql}",), (pts[hi][1],))
        for hi, h in enumerate(hs):
            tt(NT0[h, qsl, :].rearrange("p a b -> p (a b)"), pts[hi][0][h, 0:256], MASKT[h, 0:256], ALU.mult,
               (pts[hi][1], "CST"), (K("NT0", hi),))
        yield
        A_prev = lambda ql, h: NB[h, ql, 0:64]
        B_prev = lambda ql, h: NT0[h, ql, :]
        prev_keys = lambda hi: (K("NB", hi), K("NT0", hi))
        for lv in range(5):
            banks = [nb() for _ in hs]
            for i4, ql in enumerate(qls):
                for hi, h in enumerate(hs):
                    s_, sn_ = banks[hi]
                    o = i4 * 128
                    mm(s_[h, o:o + 64], B_prev(ql, h), A_prev(ql, h), True, True, prev_keys(hi), (sn_,))
                    mm(s_[h, o + 64:o + 128], A_prev(ql, h), B_prev(ql, h), True, True, prev_keys(hi), (sn_,))
            for hi, h in enumerate(hs):
                s_, sn_ = banks[hi]
                cp[(hi + sub) % 2](NN[lv][h, qsl, :].rearrange("p a b -> p (a b)"), s_[h, :], (sn_,), (K(f"NN{lv}", hi),))
                pA = NB[h, qsl, 0:64] if lv == 0 else NN[lv - 1][h, qsl, 0:64]
                pk = K("NB", hi) if lv == 0 else K(f"NN{lv - 1}", hi)
                ptt(pA, pA, IDENT4[h, :].rearrange("p (a b) -> p a b", a=4), ALU.add, (pk, "CST"), (pk,))
                if lv == 4:
                    pA = NN[4][h, qsl, 0:64]
                    ptt(pA, pA, IDENT4[h, :].rearrange("p (a b) -> p a b", a=4), ALU.add, (K("NN4", hi), "CST"), (K("NN4", hi),))
            A_prev = (lambda lv_: (lambda ql, h: NN[lv_][h, ql, 0:64]))(lv)
            B_prev = (lambda lv_: (lambda ql, h: NN[lv_][h, ql, 64:128]))(lv)
            prev_keys = (lambda lv_: (lambda hi: (K(f"NN{lv_}", hi),)))(lv)
            yield
        zps = [nb() for _ in hs]
        for i4, ql in enumerate(qls):
            q = g * GP + ql
            for hi, h in enumerate(hs):
                zp, zpn = zps[hi]
                mm(zp[h, i4 * 64:(i4 + 1) * 64], AR[h, ql, c, 0:64], SBF[h, q, :], True, False, (f"OA{ql}", K("SBF", hi)), (zpn,))
                mm(zp[h, i4 * 64:(i4 + 1) * 64], NK[h, ql, 0:64], TPS[h, ql, 2, :], False, True, (K("NK", hi), K("TPS", hi)), (zpn,))
        for hi, h in enumerate(hs):
            cp[(hi + sub) % 2](XA[0][h, qsl, :].rearrange("p a b -> p (a b)"), zps[hi][0][h, 0:256], (zps[hi][1],), (K("XA0", hi),))
        yield
        cur = 0
        for lv in range(6):
            aps = [nb() for _ in hs]
            for i4, ql in enumerate(qls):
                for hi, h in enumerate(hs):
                    A_l = NB[h, ql, 0:64] if lv == 0 else NN[lv - 1][h, ql, 0:64]
                    kk_ = (K("NB", hi),) if lv == 0 else (K(f"NN{lv - 1}", hi),)
                    mm(aps[hi][0][h, i4 * 64:(i4 + 1) * 64], A_l, XA[cur][h, ql, :], True, True, kk_ + (K(f"XA{cur}", hi),), (aps[hi][1],))
            for hi, h in enumerate(hs):
                cp[(hi + sub + lv) % 2](XA[1 - cur][h, qsl, :].rearrange("p a b -> p (a b)"), aps[hi][0][h, 0:256],
                                        (aps[hi][1],), (K(f"XA{1 - cur}", hi),))
            cur = 1 - cur
            yield
        UT = XA[cur]
        utk = lambda hi: K(f"XA{cur}", hi)
        if own:
            yps = [nb() for _ in hs]
            for i4, ql in enumerate(qls):
                q = g * GP + ql
                for hi, h in enumerate(hs):
                    yp, ypn = yps[hi]
                    o = slice(i4 * 64, (i4 + 1) * 64)
                    mm(yp[h, o], SBF[h, q, :], AR[h, ql, c, 64:128], True, False, (K("SBF", hi), f"OA{ql}"), (ypn,))
                    mm(yp[h, o], UT[h, ql, :], NB[h, ql, 64:128], False, False, (utk(hi), K("NB", hi)), (ypn,))
                    mm(yp[h, o], TPS[h, ql, 2, :], NK[h, ql, 64:128], False, True, (K("TPS", hi), K("NK", hi)), (ypn,))
            for hi, h in enumerate(hs):
                cp[(hi + sub) % 2](YG[h, qsl, cs], yps[hi][0][h, 0:256].rearrange("p (a b) -> p a b", a=4), (yps[hi][1],), (K("YG", hi),))
        sps = [nb() for _ in hs]
        for i4, ql in enumerate(qls):
            for hi, h in enumerate(hs):
                sp_, spn = sps[hi]
                o = slice(i4 * 64, (i4 + 1) * 64)
                mm(sp_[h, o], TPS[h, ql, 0, :], UT[h, ql, :], True, False, (K("TPS", hi), utk(hi)), (spn,))
                mm(sp_[h, o], TPS[h, ql, 1, :], TPS[h, ql, 2, :], False, True, (K("TPS", hi),), (spn,))
        for hi, h in enumerate(hs):
            sp_, spn = sps[hi]
            for i4, ql in enumerate(qls):
                q = g * GP + ql
                stt(S32[h, q, :], S32[h, q, :], GC[h, ql, c:c + 1], sp_[h, i4 * 64:(i4 + 1) * 64], ALU.mult, ALU.add,
                    (K("S32", hi), f"GC{ql}", spn), (K("S32", hi),))
            q0 = g * GP + 4 * sub
            acopy(SBF[h, q0:q0 + 4, :], S32[h, q0:q0 + 4, :], (K("S32", hi),), (K("SBF", hi),))
        yield

    def post_pair(q, ql):
        par = lambda j: PA[:, q, j:j + 1]
        Y = YG[:, ql, :]
        acopy(YB[:], Y, ("YG00", "YG01", "YG10", "YG11"), ("YB",))
        act(Y2B[:], Y, AF.Square, ("YG00", "YG01", "YG10", "YG11"), ("Y2B",))
        pm, pmn = nb()
        mm(pm[:, 0:TT], BONES, YB[:], True, True, ("CST", "YB"), (pmn,))
        pe, pen = nb()
        mm(pe[:, 0:TT], BONES, Y2B[:], True, True, ("CST", "Y2B"), (pen,))
        act(M2[:], pm[:, 0:TT], AF.Square, (pmn,), ("M2",))
        tt(VAR[:], pe[:, 0:TT], M2[:], ALU.subtract, (pen, "M2"), ("VAR",))
        ts(VAR[:], VAR[:], 0.0, GN_EPS, ALU.max, ALU.add, ("VAR",), ("VAR",))
        act(SD[:], VAR[:], AF.Sqrt, ("VAR",), ("SD",))
        recip(RS[:], SD[:], ("SD",), ("RS",))
        tt(TQ[:], Y, pm[:, 0:TT], ALU.subtract, ("YG00", "YG01", "YG10", "YG11", pmn), ("TQ",))
        yield
        ptt(TQ[:], TQ[:], RS[:], ALU.mult, ("TQ", "RS"), ("TQ",))
        pts(TQ[:], TQ[:], par(8), par(9), ALU.mult, ALU.add, ("TQ", "PA"), ("TQ",))
        yield
        ptt(TQ[:], TQ[:], BON[:, ql, :], ALU.add, ("TQ", f"BON{ql}"), ("TQ",))
        ptt(TQ[:], TQ[:], GG[:, ql, :], ALU.mult, ("TQ", f"GG{ql}"), ("TQ",))
        yield
        bk, bkk = gemm16(T_CONV(q, 1), xb_rhs, ("XB",))
        acopy(CCS[:], bk[:, 0:TT], (bkk,), ("CCS",))
        yield
        bk, bkk = gemm16(T_CONV(q, 2), xb_rhs, ("XB",))
        pcopy(UC[:, 0:2], UCR[:, q, :], (f"UCR{q}",), ("UC",))
        tt(UC[:, 2:TT + 2], CCS[:], bk[:, 0:TT], ALU.mult, ("CCS", bkk), ("UC",))
        pcopy(UCR[:, q, :], UC[:, TT:TT + 2], ("UC",), (f"UCR{q}",))
        yield
        pts(ACC[:], UC[:, 0:TT], par(10), 1.0, ALU.mult, ALU.mult, ("UC", "PA"), ("ACC",))
        stt(ACC[:], UC[:, 1:TT + 1], par(11), ACC[:], ALU.mult, ALU.add, ("UC", "PA", "ACC"), ("ACC",))
        stt(ACC[:], UC[:, 2:TT + 2], par(12), ACC[:], ALU.mult, ALU.add, ("UC", "PA", "ACC"), ("ACC",))
        yield
        bk, bkk = gemm16(T_CONV(q, 0), xb_rhs, ("XB",))
        tt(YC[:], bk[:, 0:TT], ACC[:], ALU.mult, (bkk, "ACC"), ("YC",))
        yield
        bk, bkk = gemm16(T_CONV(q, 3), xb_rhs, ("XB",))
        act(SGC[:], bk[:, 0:TT], AF.Sigmoid, (bkk,), ("SGC",))
        yield
        ptt(YC[:], YC[:], SGC[:], ALU.mult, ("YC", "SGC"), ("YC",))
        bk, bkk = gemm16(T_CONV(q, 4), xb_rhs, ("XB",))
        act(SGR[:], bk[:, 0:TT], AF.Sigmoid, (bkk,), ("SGR",))
        yield
        ptt(TQ[:], TQ[:], SGR[:], ALU.mult, ("TQ", "SGR"), ("TQ",))
        ptt(MT[:, q, :], TQ[:], YC[:], ALU.add, ("TQ", "YC"), (f"MT{q}",))
        yield

    def layer_norm(gi, bi, want_bf):
        pm, pmn = nb()
        pe, pen = nb()
        for m in range(NKC):
            acopy(HB[:, m % 2, :], R[:, m, :], (f"R{m}",), (f"HB{m % 2}",))
            act(HS[:, m % 2, :], R[:, m, :], AF.Square, (f"R{m}",), (f"HS{m % 2}",))
            mm(pm[:, 0:TT], LONES, HB[:, m % 2, :], m == 0, m == NKC - 1, ("CST", f"HB{m % 2}"), (pmn,))
            mm(pe[:, 0:TT], LONES, HS[:, m % 2, :], m == 0, m == NKC - 1, ("CST", f"HS{m % 2}"), (pen,))
        act(M2[:], pm[:, 0:TT], AF.Square, (pmn,), ("M2",))
        acopy(MEAN[:], pm[:, 0:TT], (pmn,), ("MEAN",))
        tt(VAR[:], pe[:, 0:TT], M2[:], ALU.subtract, (pen, "M2"), ("VAR",))
        ts(VAR[:], VAR[:], 0.0, LN_EPS, ALU.max, ALU.add, ("VAR",), ("VAR",))
        act(SD[:], VAR[:], AF.Sqrt, ("VAR",), ("SD",))
        recip(RS[:], SD[:], ("SD",), ("RS",))
        for m in range(NKC):
            (tt if m % 2 == 0 else ptt)(R[:, m, :], R[:, m, :], MEAN[:], ALU.subtract, (f"R{m}", "MEAN"), (f"R{m}",))
            (tt if m % 2 == 0 else ptt)(R[:, m, :], R[:, m, :], RS[:], ALU.mult, (f"R{m}", "RS"), (f"R{m}",))
            (ts if m % 2 == 0 else pts)(R[:, m, :], R[:, m, :], PLN[:, gi, m:m + 1], PLN[:, bi, m:m + 1], ALU.mult, ALU.add, (f"R{m}", "PLN"), (f"R{m}",))
            if want_bf:
                acopy(X1B[:, m, :], R[:, m, :], (f"R{m}",), ("X1B",))

    def ffn_tile(i):
        mt_rhs = lambda kc: MT[:, kc, :]
        for m in range(NKC):
            bk, bkk = gemm16(T_WO(m), mt_rhs, tuple(f"MT{q}" for q in range(NPAIR)))
            stt(R[:, m, :], R[:, m, :], ALPHA, bk[:, 0:TT], ALU.mult, ALU.add, (f"R{m}", bkk), (f"R{m}",))
        layer_norm(0, 1, True)
        x1_rhs = lambda kc: X1B[:, kc, :]
        for f in range(NFC):
            bg, bgn = gemm16(T_GU(f, 0), x1_rhs, ("X1B",))
            bu, bun = gemm16(T_GU(f, 1), x1_rhs, ("X1B",))
            sg, sgn = SG[f % 2], f"SG{f % 2}"
            act(sg[:], bg[:, 0:TT], AF.Silu, (bgn,), (sgn,))
            tt(ACTB[:, f, :], sg[:], bu[:, 0:TT], ALU.mult, (sgn, bun), (f"ACTB{f}",) + OAK)
        for m in range(NKC):
            bk, bkk = nb()
            for qd in range(4):
                wi = (m * 4 + qd) % 3
                dma('sp', WDB[wi][:].rearrange("p a b -> p (a b)"), wdb[m][:, qd * 11 * 128:(qd + 1) * 11 * 128],
                    (f"wdb{m}",), (f"WDB{wi}",))
                for f11 in range(11):
                    f = qd * 11 + f11
                    mm(bk[:, 0:TT], WDB[wi][:, f11, :], ACTB[:, f, :], f == 0, f == NFC - 1, (f"WDB{wi}", f"ACTB{f}") + OAK, (bkk,))
            stt(R[:, m, :], R[:, m, :], ALPHA, bk[:, 0:TT], ALU.mult, ALU.add, (f"R{m}", bkk), (f"R{m}",))
        layer_norm(2, 3, False)
        dma('sp', outT[i], R[:].rearrange("p a b -> p (a b)"), RK, ("outT",))

    for i in range(NT):
        do_tile(xp, i, False, last_pre=(i == NT - 1))
    for i in range(NT):
        do_tile(xo, i, True)

    P.emit(nc, es)
    es.close()
    nc.all_engine_barrier()
    for sm in P.all_sems:
        nc.gpsimd.sem_clear(sm)
    nc.all_engine_barrier()
    return nc


def _wtile(w, cols):
    t = np.zeros((128, NKC, 128), np.float32)
    sub = w[:, cols]
    t[:, :, :sub.shape[1]] = sub.reshape(NKC, 128, -1).transpose(1, 0, 2)
    return t.reshape(128, NKC * 128)


def _host_weights(inp):
    w_in = np.asarray(inp["w_in"][0], np.float32)
    w_o = np.asarray(inp["w_o"][0], np.float32)
    w_gu = np.asarray(inp["w_gu"][0], np.float32)
    w_down = np.asarray(inp["w_down"][0], np.float32)
    wa = np.zeros((N_WA, 128, 2048), np.float32)
    lo = RWKV_LO + 3 * D
    wa[T_WD] = _wtile(w_in, np.arange(lo, lo + 96))
    wa[T_AD] = _wtile(w_in, np.arange(lo + 96, lo + 192))
    wa[T_GD0] = _wtile(w_in, np.arange(lo + 192, lo + 320))
    wa[T_GD1] = _wtile(w_in, np.arange(lo + 320, lo + 448))
    for q in range(NPAIR):
        ch = np.arange(128 * q, 128 * q + 128)
        for j in range(3):
            wa[T_RKV(q, j)] = _wtile(w_in, RWKV_LO + j * D + ch)
        wa[T_CONV(q, 0)] = _wtile(w_in, ch)
        wa[T_CONV(q, 1)] = _wtile(w_in, D + ch)
        wa[T_CONV(q, 2)] = _wtile(w_in, 2 * D + ch)
        wa[T_CONV(q, 3)] = _wtile(w_in, RWKV_HI + ch)
        wa[T_CONV(q, 4)] = _wtile(w_in, RWKV_HI + D + ch)
    for m in range(16):
        wa[T_WO(m)] = _wtile(w_o, np.arange(128 * m, 128 * m + 128))
    for f in range(NFC):
        wa[T_GU(f, 0)] = _wtile(w_gu, np.arange(128 * f, 128 * f + 128))
        wa[T_GU(f, 1)] = _wtile(w_gu, DFF + np.arange(128 * f, 128 * f + 128))
    wdn = np.ascontiguousarray(
        w_down.reshape(NFC, 128, 16, 128).transpose(2, 1, 0, 3)).reshape(16, 128, DFF)
    mu = np.asarray(inp["shift_mu"][0], np.float32)
    pa = np.zeros((128, NPAIR, NPAR), np.float32)
    def pc(v):
        return np.asarray(v, np.float32).reshape(NPAIR, 128).T
    pa[:, :, 0] = pc(mu[0:D]); pa[:, :, 1] = pc(mu[D:2 * D]); pa[:, :, 2] = pc(mu[2 * D:3 * D])
    pa[:, :, 3] = pc(inp["w0"][0]); pa[:, :, 4] = pc(inp["a0"][0]); pa[:, :, 5] = pc(inp["k_k"][0])
    pa[:, :, 6] = pc(inp["k_a"][0]); pa[:, :, 7] = pc(np.asarray(inp["r_k"][0]).reshape(-1))
    pa[:, :, 8] = pc(inp["gn_g"][0]); pa[:, :, 9] = pc(inp["gn_b"][0])
    cw = np.asarray(inp["conv_w"][0], np.float32)
    pa[:, :, 10] = pc(cw[0]); pa[:, :, 11] = pc(cw[1]); pa[:, :, 12] = pc(cw[2])
    pl = np.zeros((128, 4), np.float32)
    pl[:96, 0] = mu[3 * D:3 * D + 96]; pl[:96, 1] = mu[3 * D + 96:3 * D + 192]
    pl[:, 2] = mu[3 * D + 192:3 * D + 320]; pl[:, 3] = mu[3 * D + 320:3 * D + 448]
    pln = np.zeros((128, 4, 16), np.float32)
    for j, k in enumerate(("ln1_g", "ln1_b", "ln2_g", "ln2_b")):
        pln[:, j, :] = np.asarray(inp[k][0], np.float32).reshape(16, 128).T
    lup = np.zeros((128, 4, NPAIR, 128), np.float32)
    lup[:96, 0] = np.asarray(inp["w_up"][0], np.float32).reshape(96, NPAIR, 128)
    lup[:96, 1] = np.asarray(inp["a_up"][0], np.float32).reshape(96, NPAIR, 128)
    gu = np.asarray(inp["g_up"][0], np.float32).reshape(2, 128, NPAIR, 128)
    lup[:, 2] = gu[0]; lup[:, 3] = gu[1]
    cst = np.zeros((128, 128 * 3 + 512 + 512 + TT + 256), np.float32)
    cst[:, 0:128] = np.eye(128)
    blk = np.zeros((128, 128)); blk[:64, :64] = 1.0 / 64; blk[64:, 64:] = 1.0 / 64
    cst[:, 128:256] = blk
    cst[:, 256:384] = 1.0 / 2048
    s = (np.arange(128) % 64)[:, None]; t = np.arange(64)[None, :]
    mp = np.concatenate([(s < t), (s <= t)], 1).astype(np.float32)
    cst[:, 384:896] = np.tile(mp, (1, 4))
    cst[:, 896:1408] = np.tile((t < s).astype(np.float32), (1, 8))
    rs_ = np.ones(TT, np.float32); rs_[::64] = 0.0
    cst[:, 1408:1408 + TT] = rs_[None, :]
    cst[:, 1408 + TT:1408 + TT + 256] = np.tile(np.eye(64, dtype=np.float32)[np.arange(128) % 64], (1, 4))
    return dict(wa=wa, wdn=wdn, pa=pa.reshape(128, -1), pl=pl, pln=pln.reshape(128, -1),
                lup=lup.reshape(128, -1), cst=cst.astype(ml_dtypes.bfloat16))


def _xtiles(xs):
    T = xs.shape[0]
    return np.ascontiguousarray(xs.reshape(T // TT, TT, NKC, 128).transpose(0, 3, 2, 1)).reshape(T // TT, 128, NKC * TT)


def kernel(**inputs):
    x = np.asarray(inputs["x"], np.float32)
    B, T, _ = x.shape
    half = T // 2
    NT = half // TT
    shared = _host_weights(inputs)
    nc = build_program(NT)
    in_maps = []
    for c in range(8):
        b, sh = c // 2, c % 2
        xo = _xtiles(x[b, sh * half:(sh + 1) * half])
        xp = _xtiles(x[b, 0:half]) if sh == 1 else np.zeros_like(xo)
        m = dict(shared)
        m["xo"] = xo
        m["xp"] = xp
        in_maps.append(m)
    res = run_bass_kernel_spmd(nc, in_maps, core_ids=list(range(8)))
    out = np.empty((B, T, D), np.float32)
    for c in range(8):
        b, sh = c // 2, c % 2
        o = np.asarray(res.results[c]["outT"]).reshape(NT, 128, NKC, TT)
        out[b, sh * half:(sh + 1) * half] = o.transpose(0, 3, 2, 1).reshape(half, D)
    return out
```
